# Optimizing a Trainium2 kernel written in Bass

```python
import math
import jax
import jax.numpy as jnp
from jax import lax
import numpy as np

D_MODEL = 2048
BATCH = 8
SEQ = 2048
DEPTH = 2

N_A_LAYERS = DEPTH // 2
N_B_LAYERS = DEPTH - N_A_LAYERS

NH_A = 8
DK_A = D_MODEL // 2 // NH_A
DV_A = D_MODEL // NH_A
QK_A = NH_A * DK_A
V_A = NH_A * DV_A
CONV_A = 4
CHUNK = 64
A_IN_COLS = 2 * QK_A + 3 * V_A + 2 * NH_A

NH_B = 16
DH_B = 128
W_B = NH_B * DH_B
QBLOCK = 128

ALPHA = (2.0 * DEPTH) ** 0.25
BETA = (8.0 * DEPTH) ** -0.25
LN_EPS = 1e-5

kernel_name = "yoco_mlstm_stickbreaking_hybrid"


def _layernorm(x, g, b):
    xf = x.astype(jnp.float32)
    mu = jnp.mean(xf, axis=-1, keepdims=True)
    xc = xf - mu
    var = jnp.mean(xc * xc, axis=-1, keepdims=True)
    return (xc * lax.rsqrt(var + LN_EPS) * g.astype(jnp.float32) + b.astype(jnp.float32)).astype(x.dtype)


def _causal_depthwise_conv(u, w, b):
    c = u.shape[-1]
    out = lax.conv_general_dilated(
        u, w[:, None, :], window_strides=(1,), padding=[(CONV_A - 1, 0)],
        dimension_numbers=("NWC", "WIO", "NWC"), feature_group_count=c)
    return out + b


def _mlstm_chunkwise(q, k, v, ig, lf):
    bsz, nh, s, dk = q.shape
    dv = v.shape[-1]
    nc = s // CHUNK

    def to_chunks(a):
        return jnp.moveaxis(a.reshape(bsz, nh, nc, CHUNK, *a.shape[3:]), 2, 0)

    causal = jnp.tril(jnp.ones((CHUNK, CHUNK), dtype=bool))

    def step(carry, xs):
        c_st, n_st, m_st = carry
        qc, kc, vc, ic, fc = xs
        bcum = jnp.cumsum(fc, axis=-1)
        gtot = bcum[..., -1]
        dmat = jnp.where(causal, bcum[..., :, None] - bcum[..., None, :] + ic[..., None, :], -jnp.inf)
        inter = bcum + m_st[..., None]
        m_q = jnp.maximum(inter, jnp.max(dmat, axis=-1))
        scores = jnp.einsum("bhjd,bhsd->bhjs", qc, kc) * jnp.exp(dmat - m_q[..., None])
        w_inter = jnp.exp(inter - m_q)
        num = jnp.einsum("bhjs,bhsv->bhjv", scores, vc) + w_inter[..., None] * jnp.einsum("bhjd,bhdv->bhjv", qc, c_st)
        den = jnp.sum(scores, axis=-1) + w_inter * jnp.einsum("bhjd,bhd->bhj", qc, n_st)
        h = num / jnp.maximum(jnp.abs(den), jnp.exp(-m_q))[..., None]
        wlog = gtot[..., None] - bcum + ic
        m_new = jnp.maximum(gtot + m_st, jnp.max(wlog, axis=-1))
        wk = jnp.exp(wlog - m_new[..., None])
        decay = jnp.exp(gtot + m_st - m_new)
        c_new = decay[..., None, None] * c_st + jnp.einsum("bhsd,bhsv->bhdv", kc * wk[..., None], vc)
        n_new = decay[..., None] * n_st + jnp.einsum("bhs,bhsd->bhd", wk, kc)
        return (c_new, n_new, m_new), h

    init = (jnp.zeros((bsz, nh, dk, dv), jnp.float32),
            jnp.zeros((bsz, nh, dk), jnp.float32),
            jnp.zeros((bsz, nh), jnp.float32))
    _, hs = lax.scan(step, init, (to_chunks(q), to_chunks(k), to_chunks(v), to_chunks(ig), to_chunks(lf)))
    return jnp.moveaxis(hs, 0, 2).reshape(bsz, nh, s, dv)


def _mlstm_layer(x, w_in, gate_b, conv_w, conv_b, head_g, w_out, ln_g, ln_b):
    bsz, s, _ = x.shape
    u = jnp.einsum("bsd,de->bse", x, w_in)
    qk, v, o, z, gates = jnp.split(u, [2 * QK_A, 2 * QK_A + V_A, 2 * QK_A + 2 * V_A, 2 * QK_A + 3 * V_A], axis=-1)
    qk = jax.nn.silu(_causal_depthwise_conv(qk, conv_w, conv_b))
    q, k = jnp.split(qk, 2, axis=-1)
    gates = (gates + gate_b).astype(jnp.float32)
    ig, fg = jnp.split(gates, 2, axis=-1)

    def heads(a, d):
        return a.reshape(bsz, s, NH_A, d).transpose(0, 2, 1, 3).astype(jnp.float32)

    qh = heads(q, DK_A)
    kh = heads(k, DK_A) * (DK_A ** -0.5)
    vh = heads(v, DV_A)
    ig_h = ig.transpose(0, 2, 1)
    lf_h = jax.nn.log_sigmoid(fg).transpose(0, 2, 1)
    h = _mlstm_chunkwise(qh, kh, vh, ig_h, lf_h)
    mu = jnp.mean(h, axis=-1, keepdims=True)
    hc = h - mu
    h = hc * lax.rsqrt(jnp.mean(hc * hc, axis=-1, keepdims=True) + LN_EPS)
    h = h.transpose(0, 2, 1, 3).reshape(bsz, s, V_A).astype(x.dtype) * head_g
    h = jax.nn.sigmoid(o) * h * jax.nn.silu(z)
    y = jnp.einsum("bse,ed->bsd", h, w_out)
    return _layernorm(ALPHA * x + y, ln_g, ln_b)


def _stick_breaking(q, k, v):
    s = q.shape[2]
    scale = DH_B ** -0.5
    outs = []
    for blk in range(s // QBLOCK):
        t0 = blk * QBLOCK
        t1 = t0 + QBLOCK
        qb = q[:, :, t0:t1].astype(jnp.float32)
        kp = k[:, :, :t1].astype(jnp.float32)
        vp = v[:, :, :t1].astype(jnp.float32)
        z = jnp.einsum("bhtd,bhsd->bhts", qb, kp) * scale
        mask = jnp.arange(t1)[None, :] < (t0 + jnp.arange(QBLOCK))[:, None]
        log_1mb = jnp.where(mask, jax.nn.log_sigmoid(-z), 0.0)
        between = lax.cumsum(log_1mb, axis=3, reverse=True) - log_1mb
        a = jnp.where(mask, jnp.exp(jax.nn.log_sigmoid(z) + between), 0.0)
        outs.append(jnp.einsum("bhts,bhsd->bhtd", a, vp))
    return jnp.concatenate(outs, axis=2)


def _stick_breaking_layer(x, k_sh, v_sh, w_in, w_out, ln_g, ln_b):
    bsz, s, _ = x.shape
    u = jnp.einsum("bsd,de->bse", x, w_in)
    q, z = jnp.split(u, [W_B], axis=-1)
    qh = q.reshape(bsz, s, NH_B, DH_B).transpose(0, 2, 1, 3)
    att = _stick_breaking(qh, k_sh, v_sh)
    att = att.transpose(0, 2, 1, 3).reshape(bsz, s, W_B).astype(x.dtype)
    y = jnp.einsum("bse,ed->bsd", att * jax.nn.silu(z), w_out)
    return _layernorm(ALPHA * x + y, ln_g, ln_b)


def setup_inputs(seed: int = 0) -> dict:
    key = jax.random.key(seed)
    ks = jax.random.split(key, 16)
    f32 = jnp.float32
    nrm = lambda k, shape: jax.random.normal(k, shape, f32)
    x = nrm(ks[0], (BATCH, SEQ, D_MODEL))
    a_w_in = nrm(ks[1], (N_A_LAYERS, D_MODEL, A_IN_COLS)) * D_MODEL ** -0.5
    i_bias = 0.1 * nrm(ks[2], (N_A_LAYERS, NH_A))
    f_bias = jnp.linspace(3.0, 6.0, NH_A, dtype=f32)[None, :] + 0.1 * nrm(ks[3], (N_A_LAYERS, NH_A))
    a_gate_b = jnp.concatenate([i_bias, f_bias], axis=-1)
    a_conv_w = nrm(ks[4], (N_A_LAYERS, CONV_A, 2 * QK_A)) * CONV_A ** -0.5
    a_conv_b = 0.01 * nrm(ks[5], (N_A_LAYERS, 2 * QK_A))
    a_head_g = 1.0 + 0.02 * nrm(ks[6], (N_A_LAYERS, V_A))
    a_w_out = nrm(ks[7], (N_A_LAYERS, V_A, D_MODEL)) * (V_A ** -0.5) * BETA
    a_ln_g = 1.0 + 0.02 * nrm(ks[8], (N_A_LAYERS, D_MODEL))
    a_ln_b = 0.02 * nrm(ks[9], (N_A_LAYERS, D_MODEL))
    kv_w = nrm(ks[10], (D_MODEL, 2 * W_B)) * D_MODEL ** -0.5
    b_w_in = nrm(ks[11], (N_B_LAYERS, D_MODEL, 2 * W_B)) * D_MODEL ** -0.5
    b_w_out = nrm(ks[12], (N_B_LAYERS, W_B, D_MODEL)) * (W_B ** -0.5) * BETA
    b_ln_g = 1.0 + 0.02 * nrm(ks[13], (N_B_LAYERS, D_MODEL))
    b_ln_b = 0.02 * nrm(ks[14], (N_B_LAYERS, D_MODEL))
    return {"x": x, "a_w_in": a_w_in, "a_gate_b": a_gate_b, "a_conv_w": a_conv_w, "a_conv_b": a_conv_b,
            "a_head_g": a_head_g, "a_w_out": a_w_out, "a_ln_g": a_ln_g, "a_ln_b": a_ln_b,
            "kv_w": kv_w, "b_w_in": b_w_in, "b_w_out": b_w_out, "b_ln_g": b_ln_g, "b_ln_b": b_ln_b}


def reference(x, a_w_in, a_gate_b, a_conv_w, a_conv_b, a_head_g, a_w_out, a_ln_g, a_ln_b,
              kv_w, b_w_in, b_w_out, b_ln_g, b_ln_b):
    bsz, s, _ = x.shape
    k_sh = None
    v_sh = None
    for layer in range(DEPTH):
        if layer < N_A_LAYERS:
            x = _mlstm_layer(x, a_w_in[layer], a_gate_b[layer], a_conv_w[layer], a_conv_b[layer],
                             a_head_g[layer], a_w_out[layer], a_ln_g[layer], a_ln_b[layer])
        else:
            if layer == N_A_LAYERS:
                kv = jnp.einsum("bsd,de->bse", x, kv_w)
                k_sh, v_sh = jnp.split(kv, 2, axis=-1)
                k_sh = k_sh.reshape(bsz, s, NH_B, DH_B).transpose(0, 2, 1, 3)
                v_sh = v_sh.reshape(bsz, s, NH_B, DH_B).transpose(0, 2, 1, 3)
            j = layer - N_A_LAYERS
            x = _stick_breaking_layer(x, k_sh, v_sh, b_w_in[j], b_w_out[j], b_ln_g[j], b_ln_b[j])
    return x
```

```python
import contextlib
import math
import numpy as np
import concourse.bass as bass
import concourse.mybir as mybir
from concourse.bass_utils import run_bass_kernel_spmd

F32 = mybir.dt.float32
BF16 = mybir.dt.bfloat16
AF = mybir.ActivationFunctionType
ALU = mybir.AluOpType
AX = mybir.AxisListType

S = 2048
D = 2048
KC = 16
NT = 16
NH_A = 8
NH_B = 16
A_COLS = 8208
A_COLS_PAD = 8320
ALPHA = (2.0 * 2) ** 0.25
LN_EPS = 1e-5
VW = 258


class KB:
    def __init__(self, nc, es):
        self.nc = nc
        self.es = es
        self.eng = {"pe": nc.tensor, "act": nc.scalar, "dve": nc.vector, "pool": nc.gpsimd, "sp": nc.sync}
        self.sem = {e: es.enter_context(nc.semaphore("s_" + e)) for e in ("pe", "act", "dve", "pool")}
        self.cnt = {e: 0 for e in self.sem}
        self.seen = {e: {} for e in self.eng}
        self.lastw = {}
        self.reads = {}
        self.pend = {e: ([], []) for e in self.eng}
        self.dsem = {}
        self.semobj = dict(self.sem)

    def _wait(self, e, dep):
        if dep is None:
            return
        sn, v = dep
        if self.seen[e].get(sn, 0) >= v:
            return
        self.seen[e][sn] = v
        self.eng[e].wait_ge(self.semobj[sn], v)

    def _deps(self, e, R, W):
        for k in R:
            self._wait(e, self.lastw.get(k))
        for k in W:
            self._wait(e, self.lastw.get(k))
            for d in self.reads.get(k, ()):
                self._wait(e, d)

    def _record(self, tag, R, W):
        for k in R:
            lst = self.reads.setdefault(k, [])
            lst[:] = [d for d in lst if d[0] != tag[0]]
            lst.append(tag)
        for k in W:
            self.lastw[k] = tag
            self.reads[k] = []

    @staticmethod
    def _norm(R, W):
        R2, W2 = [], []
        for k in R:
            if isinstance(k, tuple) and k[0] == "ps":
                W2.append(k[:2])
            elif k == "PT":
                W2.append(k)
            else:
                R2.append(k)
        for k in W:
            W2.append(k[:2] if (isinstance(k, tuple) and k[0] == "ps") else k)
        return R2, W2

    def op(self, e, fn, R=(), W=(), inc=True):
        R, W = self._norm(R, W)
        self._deps(e, R, W)
        ins = fn()
        if inc:
            self.cnt[e] += 1
            ins.then_inc(self.sem[e], 1)
            tag = (e, self.cnt[e])
            pr, pw = self.pend[e]
            self._record(tag, list(pr) + list(R), list(pw) + list(W))
            self.pend[e] = ([], [])
        else:
            self.pend[e][0].extend(R)
            self.pend[e][1].extend(W)
        return ins

    def dma(self, q, slot, out, in_, R=(), W=(), A=(), **kw):
        for k in A:
            self._wait(q, self.lastw.get(k))
        if slot not in self.dsem:
            s = self.es.enter_context(self.nc.semaphore("d_" + slot))
            self.dsem[slot] = [s, 0]
            self.semobj["d_" + slot] = s
        self._deps(q, R, W)
        ent = self.dsem[slot]
        ent[1] += 16
        self.eng[q].dma_start(out=out, in_=in_, **kw).then_inc(ent[0], 16)
        self._record(("d_" + slot, ent[1]), R, W)

    def barrier(self):
        for e in self.eng:
            for s in self.sem:
                if self.cnt[s] > 0:
                    self._wait(e, (s, self.cnt[s]))
            for slot, (so, c) in self.dsem.items():
                if c > 0:
                    self._wait(e, ("d_" + slot, c))


class Stop(Exception):
    pass


STOP_AT = None


def _chk(name):
    if STOP_AT == name:
        raise Stop()


def build(phase):
    nc = bass.Bass("TRN2", target_bir_lowering=False)
    dt_in = lambda name, shape: nc.dram_tensor(name, shape, F32, kind="ExternalInput").ap()
    doA = "A" in phase
    doB = "B" in phase
    x_in = dt_in("x", [S, D])
    if doA:
        a_w_in = dt_in("a_w_in", [D, A_COLS_PAD])
        a_gate_b = dt_in("a_gate_b", [1, 16])
        a_conv_w = dt_in("a_conv_w", [4, 2048])
        a_conv_b = dt_in("a_conv_b", [1, 2048])
        a_head_g = dt_in("a_head_g", [1, 2048])
        a_w_out = dt_in("a_w_out", [D, D])
        a_ln_g = dt_in("a_ln_g", [1, D])
        a_ln_b = dt_in("a_ln_b", [1, D])
    if doB:
        kv_w = dt_in("kv_w", [D, 4096])
        b_w_in = dt_in("b_w_in", [D, 4096])
        b_w_out = dt_in("b_w_out", [D, D])
        b_ln_g = dt_in("b_ln_g", [1, D])
        b_ln_b = dt_in("b_ln_b", [1, D])
    y_out = nc.dram_tensor("y", [S, D], F32, kind="ExternalOutput").ap()

    with contextlib.ExitStack() as es:
        kb = KB(nc, es)
        sb = lambda name, shape, dt: es.enter_context(nc.sbuf_tensor(name, shape, dt))
        XT = sb("XT", [128, KC, S], BF16)
        HT = sb("HT", [128, KC, S], BF16)
        WB = [sb(f"WB{i}", [128, KC, 128], BF16) for i in range(2)]
        SCR = [sb(f"SCR{i}", [128, 2056], F32) for i in range(2)]
        ID32 = sb("ID32", [128, 128], F32)
        IDB = sb("IDB", [128, 128], BF16)
        ONE32 = sb("ONE32", [128, 128], F32)
        MASKC = sb("MASKC", [128, 128], F32)
        MASKS = sb("MASKS", [128, 128], F32)
        MASKSB = sb("MASKSB", [128, 128], BF16)
        TRIU32 = sb("TRIU32", [128, 128], F32)
        NTRI = sb("NTRI", [128, 128], BF16)
        NUTRI = sb("NUTRI", [128, 128], BF16)
        STATS = sb("STATS", [128, 4, 6], F32)
        MV = sb("MV", [128, 2], F32)
        SM = sb("SM", [128, 8], F32)
        PS = [es.enter_context(nc.psum_tensor(f"PS{i}", [128, 512], F32)) for i in range(7)]
        PT = es.enter_context(nc.psum_tensor("PT", [128, 1024], BF16))

        V, G, A_, P_ = nc.vector, nc.gpsimd, nc.scalar, nc.tensor

        kb.op("pool", lambda: G.memset(ONE32[:], 1.0), W=["ONE32"])
        kb.op("pool", lambda: G.memset(SCR[0][:, 0:3], 0.0), W=["SCR0"])

        def sel(out, cmp, fill_in, base, cm, step, key):
            kb.op("pool", lambda: G.affine_select(out=out[:], in_=fill_in[:], pattern=[[step, 128]], compare_op=cmp,
                                                   fill=0.0, base=base, channel_multiplier=cm),
                  R=["ONE32"], W=[key])
        sel(ID32, ALU.is_equal, ONE32, 0, -1, 1, "ID32")
        sel(MASKC, ALU.is_ge, ONE32, 0, -1, 1, "MASKC")
        sel(MASKS, ALU.is_gt, ONE32, 0, -1, 1, "MASKS")
        sel(TRIU32, ALU.is_ge, ONE32, 0, -1, 1, "TRIU32")
        kb.op("dve", lambda: V.tensor_copy(out=IDB[:], in_=ID32[:]), R=["ID32"], W=["IDB"])
        kb.op("dve", lambda: V.tensor_copy(out=MASKSB[:], in_=MASKS[:]), R=["MASKS"], W=["MASKSB"])
        kb.op("dve", lambda: V.tensor_scalar(out=NUTRI[:], in0=MASKS[:], scalar1=-1.0, scalar2=None, op0=ALU.mult),
              R=["MASKS"], W=["NUTRI"])
        kb.op("dve", lambda: V.tensor_scalar(out=NTRI[:], in0=MASKS[:], scalar1=-1.0, scalar2=1.0, op0=ALU.add,
                                             op1=ALU.mult), R=["MASKS"], W=["NTRI"])

        def load_transpose(src, dstT, dkey):
            stg = [(SCR[0][:, 8:8 + D], "SCR0"), (SCR[1][:, 8:8 + D], "SCR1")]
            if dstT is XT:
                stg.append((HT[:, 0:2, :].rearrange("p a b -> p (a b)").bitcast(F32), "HTs0"))
                stg.append((HT[:, 2:4, :].rearrange("p a b -> p (a b)").bitcast(F32), "HTs1"))
            n = len(stg)
            for t in range(NT):
                sc, sk = stg[t % n]
                kb.dma("sp", f"xin{t % n}", sc, src[t * 128:(t + 1) * 128, :], W=[sk])
                for g in range(4):
                    bank = (t * 4 + g) % 4
                    for j in range(4):
                        c = g * 4 + j
                        kb.op("pe", lambda: P_.transpose(out=PS[bank][:, j * 128:(j + 1) * 128],
                                                         in_=sc[:, c * 128:(c + 1) * 128], identity=ID32[:]),
                              R=[sk, "ID32"], W=[("ps", bank)], inc=(j == 3))
                    dst = dstT[:, g * 4:(g + 1) * 4, t * 128:(t + 1) * 128]
                    srcv = PS[bank][:, :].rearrange("p (c t) -> p c t", c=4)
                    if (t * 4 + g) % 2 == 0:
                        kb.op("dve", lambda: V.tensor_copy(out=dst, in_=srcv), R=[("ps", bank)], W=[dkey])
                    else:
                        kb.op("act", lambda: A_.copy(out=dst, in_=srcv), R=[("ps", bank)], W=[dkey])

        wslot = [0]

        class Unit:
            def __init__(self, wsrc, col0, evac, banks):
                self.wsrc, self.col0, self.evac, self.banks = wsrc, col0, evac, banks
                self.slot = None

        def unit_load(u, after=()):
            u.slot = wslot[0] % 2
            wslot[0] += 1
            wv = u.wsrc.rearrange("(c p) n -> p c n", p=128)
            kb.dma("pool", f"wb{u.slot}", WB[u.slot][:], wv[:, :, u.col0:u.col0 + 128], A=list(after),
                   W=[f"WB{u.slot}"])

        def unit_compute(u, src, skey):
            for tb in range(4):
                bank = u.banks[tb % len(u.banks)]
                for c in range(KC):
                    kb.op("pe", lambda: P_.matmul(PS[bank][:, :], lhsT=WB[u.slot][:, c, :],
                                                  rhs=src[:, c, tb * 512:(tb + 1) * 512],
                                                  start=(c == 0), stop=(c == KC - 1)),
                          R=[f"WB{u.slot}", skey], W=[("ps", bank)], inc=(c == KC - 1))
                u.evac(tb, bank)

        def run_units(units, src, skey):
            unit_load(units[0])
            for i, u in enumerate(units):
                if i + 1 < len(units):
                    unit_load(units[i + 1])
                unit_compute(u, src, skey)

        lnid = [0]

        def load_wo(wout, buf=None, key="XT"):
            buf = XT if buf is None else buf
            wv = wout.rearrange("(c p) n -> p c n", p=128)
            for nb in range(4):
                kb.dma("pool", f"wo{nb}", buf[:, :, nb * 512:(nb + 1) * 512], wv[:, :, nb * 512:(nb + 1) * 512],
                       W=[key])

        def ln_epilogue(src_x, mid, wout, lng, lnb, dst, mkey="MID", wo_buf=None, wo_key="XT", xt_out=False):
            lnid[0] += 1
            wo_buf = XT if wo_buf is None else wo_buf
            with nc.sbuf_tensor("LNG%d" % lnid[0], [128, D], F32) as LNG, \
                    nc.sbuf_tensor("LNB%d" % lnid[0], [128, D], F32) as LNB, \
                    nc.sbuf_tensor("XB%da" % lnid[0], [128, D], BF16) as XB0, \
                    nc.sbuf_tensor("XB%db" % lnid[0], [128, D], BF16) as XB1:
                XB = [XB0, XB1]
                kb.dma("sp", "lng", LNG[:], lng.partition_broadcast(128), W=["LNG"])
                kb.dma("sp", "lnb", LNB[:], lnb.partition_broadcast(128), W=["LNB"])
                def ldx(t):
                    kb.dma("sp", f"xin{t % 2}", SCR[t % 2][:, 8:8 + D], src_x[t * 128:(t + 1) * 128, :],
                           W=[f"SCR{t % 2}"])
                pend_xt = []

                def xt_stage(tt):
                    xb, xk = XB[tt % 2], f"XB{tt % 2}"
                    for g in range(2):
                        for j in range(8):
                            c = g * 8 + j
                            kb.op("pe", lambda: P_.transpose(out=PT[:, j * 128:(j + 1) * 128],
                                                             in_=xb[:, c * 128:(c + 1) * 128], identity=IDB[:]),
                                  R=[xk, "IDB"], W=["PT"], inc=(j == 7))
                        kb.op("act", lambda: A_.copy(out=mid[:, g * 8:(g + 1) * 8, tt * 128:(tt + 1) * 128],
                                                     in_=PT[:, :].rearrange("p (a b) -> p a b", a=8)),
                              R=["PT"], W=[("X1T", tt)])

                ldx(0)
                for t in range(NT):
                    sc = SCR[t % 2]
                    sk = f"SCR{t % 2}"
                    if t + 1 < NT:
                        ldx(t + 1)
                    for nb in range(4):
                        bank = nb
                        for c in range(KC):
                            kb.op("pe", lambda: P_.matmul(PS[bank][:, :], lhsT=mid[:, c, t * 128:(t + 1) * 128],
                                                          rhs=wo_buf[:, c, nb * 512:(nb + 1) * 512],
                                                          start=(c == 0), stop=(c == KC - 1)),
                                  R=[wo_key, mkey, ("X1T", t)], W=[("ps", bank)], inc=(c == KC - 1))
                        seg = sc[:, 8 + nb * 512:8 + (nb + 1) * 512]
                        kb.op("dve", lambda: V.scalar_tensor_tensor(out=seg, in0=seg, scalar=ALPHA, in1=PS[bank][:, :],
                                                                    op0=ALU.mult, op1=ALU.add),
                              R=[sk, ("ps", bank)], W=[sk])
                        kb.op("dve", lambda: V.bn_stats(out=STATS[:, nb, :], in_=seg), R=[sk], W=["STATS"])
                    if xt_out and pend_xt:
                        xt_stage(pend_xt.pop(0))
                    kb.op("dve", lambda: V.bn_aggr(out=MV[:], in_=STATS[:].rearrange("p a b -> p (a b)")),
                          R=["STATS"], W=["MV"])
                    kb.op("act", lambda: A_.activation(out=SM[:, 0:1], in_=MV[:, 1:2], func=AF.Ln, bias=LN_EPS, scale=1.0),
                          R=["MV"], W=["SM0"])
                    kb.op("act", lambda: A_.activation(out=SM[:, 1:2], in_=SM[:, 0:1], func=AF.Exp, scale=-0.5),
                          R=["SM0"], W=["SM1"])
                    full = sc[:, 8:8 + D]
                    kb.op("dve", lambda: V.tensor_scalar(out=full, in0=full, scalar1=MV[:, 0:1], scalar2=SM[:, 1:2],
                                                         op0=ALU.subtract, op1=ALU.mult),
                          R=[sk, "MV", "SM1"], W=[sk])
                    kb.op("pool", lambda: G.tensor_tensor(out=full, in0=full, in1=LNG[:], op=ALU.mult),
                          R=[sk, "LNG"], W=[sk])
                    kb.op("dve", lambda: V.tensor_tensor(out=full, in0=full, in1=LNB[:], op=ALU.add),
                          R=[sk, "LNB"], W=[sk])
                    kb.dma("sp", f"yout{t % 2}", dst[t * 128:(t + 1) * 128, :], full, R=[sk], W=["YOUT"])
                    if xt_out:
                        xb, xk = XB[t % 2], f"XB{t % 2}"
                        kb.op("act", lambda: A_.copy(out=xb[:, :], in_=full), R=[sk], W=[xk])
                        pend_xt.append(t)
                if xt_out:
                    xt_stage(pend_xt.pop(0))
            kb.barrier()

        def phase_a(x_src, dst, xt_out=False):
            stopped = []
            with contextlib.ExitStack() as ea:
              try:
                sa = lambda name, shape, dt: ea.enter_context(nc.sbuf_tensor(name, shape, dt))
                QT = sa("QT", [128, S], BF16)
                KT = sa("KT", [128, S], BF16)
                VT = sa("VT", [128, S], BF16)
                VX = sa("VX", [128, NT, VW], BF16)
                GT = sa("GT", [128, 2, S], BF16)
                PRT = sa("PRT", [128, 96], F32)
                WW = sa("WW", [128, 128], F32)
                FL = sa("FL", [128, 128], F32)
                DB = sa("DB", [128, 128], F32)
                GW = [sa(f"GW{i}", [128, KC, 128], BF16) for i in range(4)]
                eg = contextlib.ExitStack()
                sg = lambda name, shape, dt: eg.enter_context(nc.sbuf_tensor(name, shape, dt))
                WG = sg("WG", [128, KC, 16], BF16)
                PR = sg("PR", [96, 128], F32)
                GB = sg("GB", [128, 16], F32)
                GI = sg("GI", [128, 128], F32)
                SPF = sg("SPF", [128, 128], F32)
                BP = sg("BP", [128, 128], F32)
                AA = sg("AA", [128, 128], F32)
                TMP = sg("TMP", [128, 128], F32)
                ROW = SCR[1][0:1, 0:512].rearrange("p (a b) -> p a b", a=4)
                MROW = SCR[1][0:1, 512:648]

                kb.dma("sp", "pr0", PR[0:64, :], a_conv_w.rearrange("k (c p) -> (k c) p", p=128), W=["PR"])
                kb.dma("sp", "pr1", PR[64:80, :], a_conv_b.rearrange("o (c p) -> (o c) p", p=128), W=["PR"])
                kb.dma("sp", "pr2", PR[80:96, :], a_head_g.rearrange("o (c p) -> (o c) p", p=128), W=["PR"])
                kb.dma("sp", "gb", GB[:], a_gate_b.partition_broadcast(128), W=["GB"])
                wv = a_w_in.rearrange("(c p) n -> p c n", p=128)
                kb.dma("pool", "wg", WG[:], wv[:, :, 8192:8208], W=["WG"])
                vun0 = [Unit(a_w_in, 2048, None, None), Unit(a_w_in, 2048 + 128, None, None)]
                unit_load(vun0[0], after=["WG"])
                unit_load(vun0[1], after=["WG"])
                load_transpose(x_src, XT, "XT")
                kb.barrier()
                _chk("prologue")
                for gi, c0 in enumerate((4096, 6144, 4096 + 128, 6144 + 128)):
                    kb.dma("pool", f"gw{gi}", GW[gi][:], wv[:, :, c0:c0 + 128], W=[f"GW{gi}"])
                kb.op("pe", lambda: P_.transpose(out=PS[0][:, 0:96], in_=PR[:, :], identity=ID32[0:96, 0:96]),
                      R=["PR", "ID32"], W=[("ps", 0)])
                kb.op("dve", lambda: V.tensor_copy(out=PRT[:], in_=PS[0][:, 0:96]), R=[("ps", 0)], W=["PRT"])
                kb.op("pool", lambda: G.memset(VX[:, :, 256:258], 1.0), W=["VX"])
                _chk("params")

                for t in range(NT):
                    for c in range(KC):
                        kb.op("pe", lambda: P_.matmul(PS[1][:, t * 16:(t + 1) * 16], lhsT=XT[:, c, t * 128:(t + 1) * 128],
                                                      rhs=WG[:, c, :], start=(c == 0), stop=(c == KC - 1)),
                              R=["XT", "WG"], W=[("ps", 1)], inc=(c == KC - 1 and t == NT - 1))
                g3 = PS[1][:, 0:256].rearrange("p (t g) -> p t g", g=16)
                gi3 = GI[:].rearrange("p (t h) -> p t h", h=8)
                sp3 = SPF[:].rearrange("p (t h) -> p t h", h=8)
                kb.op("dve", lambda: V.tensor_tensor(out=gi3, in0=g3[:, :, 0:8],
                                                     in1=GB[:, 0:8].unsqueeze(1).broadcast_to([128, NT, 8]), op=ALU.add),
                      R=[("ps", 1), "GB"], W=["GI"])
                kb.op("dve", lambda: V.tensor_tensor(out=sp3, in0=g3[:, :, 8:16],
                                                     in1=GB[:, 8:16].unsqueeze(1).broadcast_to([128, NT, 8]), op=ALU.add),
                      R=[("ps", 1), "GB"], W=["SPF"])
                kb.op("act", lambda: A_.activation(out=TMP[:], in_=SPF[:], func=AF.Exp, scale=-1.0), R=["SPF"], W=["TMP"])
                kb.op("act", lambda: A_.activation(out=SPF[:], in_=TMP[:], func=AF.Ln, bias=1.0, scale=1.0),
                      R=["TMP"], W=["SPF"])
                kb.op("pe", lambda: P_.matmul(PS[2][:, 0:128], lhsT=TRIU32[:], rhs=SPF[:], start=True, stop=True),
                      R=["TRIU32", "SPF"], W=[("ps", 2)])
                kb.op("dve", lambda: V.tensor_copy(out=BP[:], in_=PS[2][:, 0:128]), R=[("ps", 2)], W=["BP"])
                kb.op("dve", lambda: V.tensor_tensor(out=AA[:], in0=GI[:], in1=BP[:], op=ALU.add), R=["GI", "BP"], W=["AA"])
                kb.op("pe", lambda: P_.transpose(out=PS[3][:, 0:128], in_=AA[:], identity=ID32[:]),
                      R=["AA", "ID32"], W=[("ps", 3)])
                kb.op("dve", lambda: V.tensor_reduce(out=SM[:, 2:3], in_=PS[3][:, 0:128], axis=AX.X, op=ALU.max),
                      R=[("ps", 3)], W=["SM2"])
                kb.op("pe", lambda: P_.matmul(PS[0][0:1, 0:128], lhsT=SM[:, 2:3], rhs=ID32[:], start=True, stop=True),
                      R=["SM2", "ID32"], W=[("ps", 0)])
                kb.op("pe", lambda: P_.matmul(PS[0][0:1, 128:256], lhsT=ONE32[:, 0:1], rhs=SPF[:], start=True, stop=True),
                      R=["ONE32", "SPF"], W=[("ps", 0)])
                kb.op("dve", lambda: V.tensor_copy(out=ROW[:, 0:2, :], in_=PS[0][0:1, 0:256].rearrange("p (a b) -> p a b", a=2)),
                      R=[("ps", 0)], W=["ROW"])
                kb.op("dve", lambda: V.memset(MROW[:], 0.0), W=["MROW"])
                for t in range(NT):
                    sl = slice(t * 8, (t + 1) * 8)
                    sl1 = slice((t + 1) * 8, (t + 2) * 8)
                    kb.op("dve", lambda: V.tensor_tensor(out=ROW[:, 2, sl], in0=MROW[:, sl], in1=ROW[:, 0, sl], op=ALU.max),
                          R=["MROW", "ROW"], W=["ROW"])
                    kb.op("dve", lambda: V.tensor_tensor(out=MROW[:, sl1], in0=ROW[:, 2, sl], in1=ROW[:, 1, sl],
                                                         op=ALU.subtract), R=["ROW"], W=["MROW"])
                kb.op("dve", lambda: V.tensor_tensor(out=ROW[:, 3, :], in0=MROW[:, 0:128], in1=ROW[:, 2, :], op=ALU.subtract),
                      R=["MROW", "ROW"], W=["ROW"])
                kb.op("pe", lambda: P_.matmul(PS[2][:, 0:128], lhsT=ONE32[0:1, :], rhs=ROW[:, 2, :], start=True, stop=True),
                      R=["ONE32", "ROW"], W=[("ps", 2)])
                kb.op("pe", lambda: P_.matmul(PS[2][:, 128:256], lhsT=ONE32[0:1, :], rhs=ROW[:, 3, :], start=True, stop=True),
                      R=["ONE32", "ROW"], W=[("ps", 2)])
                kb.op("dve", lambda: V.tensor_tensor(out=TMP[:], in0=AA[:], in1=PS[2][:, 0:128], op=ALU.subtract),
                      R=["AA", ("ps", 2)], W=["TMP"])
                kb.op("act", lambda: A_.activation(out=WW[:], in_=TMP[:], func=AF.Exp, bias=-0.5 * math.log(128.0), scale=1.0),
                      R=["TMP"], W=["WW"])
                kb.op("dve", lambda: V.tensor_tensor(out=TMP[:], in0=BP[:], in1=PS[2][:, 0:128], op=ALU.subtract),
                      R=["BP", ("ps", 2), "WW"], W=["TMP"])
                kb.op("act", lambda: A_.activation(out=FL[:], in_=TMP[:], func=AF.Exp), R=["TMP"], W=["FL"])
                kb.op("act", lambda: A_.activation(out=DB[:], in_=PS[2][:, 128:256], func=AF.Exp), R=[("ps", 2)], W=["DB"])
                kb.barrier()
                eg.close()
                C32 = sa("C32", [128, VW], F32)
                CB = sa("CB", [128, VW], BF16)
                STM = [sa(f"STM{i}", [128, 128], BF16) for i in range(2)]
                KP = [sa(f"KP{i}", [128, 128], BF16) for i in range(2)]
                HN = [sa(f"HN{i}", [128, 256], F32) for i in range(2)]
                SS = sa("SS", [128, 24], F32)
                BST = sa("BST", [128, 18], F32)
                BMV = sa("BMV", [128, 6], F32)

                _chk("gates")
                U = SCR[0]
                ACC = SCR[1]

                def conv_block(ch, tb, dstT, dkey):
                    wk = lambda k: PRT[:, k * 16 + ch:k * 16 + ch + 1]
                    cbias = PRT[:, 64 + ch:64 + ch + 1]
                    accb = ACC[:, tb * 512:(tb + 1) * 512]
                    ub = lambda k: U[:, k + tb * 512:k + (tb + 1) * 512]
                    ur = ["SCR0", "SCR1", ("U", tb), "PRT"] + ([("U", tb - 1)] if tb > 0 else [])
                    kb.op("dve", lambda: V.tensor_scalar(out=accb, in0=ub(3), scalar1=wk(3), scalar2=cbias,
                                                         op0=ALU.mult, op1=ALU.add), R=ur, W=[("ACC", tb)])
                    for k in (2, 1, 0):
                        kb.op("dve", lambda: V.scalar_tensor_tensor(out=accb, in0=ub(k), scalar=wk(k), in1=accb,
                                                                    op0=ALU.mult, op1=ALU.add),
                              R=ur + [("ACC", tb)], W=[("ACC", tb)])

                def conv_silu(tb, dstT, dkey):
                    kb.op("act", lambda: A_.activation(out=dstT[:, tb * 512:(tb + 1) * 512],
                                                       in_=ACC[:, tb * 512:(tb + 1) * 512], func=AF.Silu),
                          R=["SCR1", ("ACC", tb)], W=[dkey])

                banksA = [0, 1, 2, 3]
                vnext = [None]
                VS = [VT[:, :], SCR[0][:, 8:1032].bitcast(BF16)]
                vskey = ["VT", "SCR0"]

                def v_gen(vun):
                    nbk = 0
                    for half in range(2):
                        u = vun[half]
                        for tb in range(4):
                            bank = (4, 1)[nbk % 2]
                            nbk += 1
                            for c in range(KC):
                                kb.op("pe", lambda: P_.matmul(PS[bank][:, :], lhsT=WB[u.slot][:, c, :],
                                                              rhs=XT[:, c, tb * 512:(tb + 1) * 512],
                                                              start=(c == 0), stop=(c == KC - 1)),
                                      R=[f"WB{u.slot}", "XT"], W=[("ps", bank)], inc=(c == KC - 1))
                                if c % 4 == 3:
                                    yield None
                            kb.op("act", lambda: A_.copy(out=VS[half][:, tb * 512:(tb + 1) * 512], in_=PS[bank][:, :]),
                                  R=[("ps", bank)], W=[vskey[half]])

                def v_transposes():
                    for half in range(2):
                        for g in range(2):
                            for j in range(8):
                                t = g * 8 + j
                                kb.op("pe", lambda: P_.transpose(out=PT[:, j * 128:(j + 1) * 128],
                                                                 in_=VS[half][:, t * 128:(t + 1) * 128], identity=IDB[:]),
                                      R=[vskey[half], "IDB"], W=["PT"], inc=(j == 7))
                            kb.op("dve", lambda: V.tensor_copy(
                                out=VX[:, g * 8:(g + 1) * 8, half * 128:(half + 1) * 128],
                                in_=PT[:, :].rearrange("p (a b) -> p a b", a=8)), R=["PT"], W=["VX"])
                    kb.op("pool", lambda: G.memset(SCR[0][:, 0:3], 0.0), W=["SCR0"])

                def gen_pull(g, n):
                    for _ in range(n):
                        if g[0] is None:
                            return
                        try:
                            next(g[0])
                        except StopIteration:
                            g[0] = None
                            return

                vgen = [None]
                gwpre = [False]
                for h in range(NH_A):
                    def ev_q(tb, bank):
                        kb.op("act", lambda: A_.copy(out=U[:, 3 + tb * 512:3 + (tb + 1) * 512], in_=PS[bank][:, :]),
                              R=[("ps", bank), "SCR0"], W=[("U", tb)])
                        conv_block(h, tb, QT, "QT")
                        if tb > 0:
                            conv_silu(tb - 1, QT, "QT")
                        if tb == 3:
                            conv_silu(3, QT, "QT")

                    def ev_k(tb, bank):
                        kb.op("act", lambda: A_.copy(out=U[:, 3 + tb * 512:3 + (tb + 1) * 512], in_=PS[bank][:, :]),
                              R=[("ps", bank), "SCR0"], W=[("U", tb)])
                        conv_block(8 + h, tb, KT, "KT")
                        if tb > 0:
                            conv_silu(tb - 1, KT, "KT")
                        if tb == 3:
                            conv_silu(3, KT, "KT")

                    def mk_ev_v(half):
                        def ev(tb, bank):
                            kb.op("act", lambda: A_.copy(out=VT[:, tb * 512:(tb + 1) * 512], in_=PS[bank][:, :]),
                                  R=[("ps", bank)], W=["VT"])
                            if tb == 3:
                                for g in range(2):
                                    for j in range(8):
                                        t = g * 8 + j
                                        kb.op("pe", lambda: P_.transpose(out=PT[:, j * 128:(j + 1) * 128],
                                                                         in_=VT[:, t * 128:(t + 1) * 128], identity=IDB[:]),
                                              R=["VT", "IDB"], W=["PT"], inc=(j == 7))
                                    kb.op("dve", lambda: V.tensor_copy(
                                        out=VX[:, g * 8:(g + 1) * 8, half * 128:(half + 1) * 128],
                                        in_=PT[:, :].rearrange("p (a b) -> p a b", a=8)), R=["PT"], W=["VX"])
                        return ev

                    wv_a = a_w_in.rearrange("(c p) n -> p c n", p=128)
                    gcols = (4096 + h * 256, 6144 + h * 256, 4096 + h * 256 + 128, 6144 + h * 256 + 128)

                    def gw_load(gi, after=()):
                        kb.dma("pool", f"gw{gi}", GW[gi][:], wv_a[:, :, gcols[gi]:gcols[gi] + 128],
                               A=list(after), W=[f"GW{gi}"])

                    mkv = lambda hh: [Unit(a_w_in, 2048 + hh * 256, None, banksA),
                                      Unit(a_w_in, 2048 + hh * 256 + 128, None, banksA)]
                    uq = Unit(a_w_in, h * 128, ev_q, banksA)
                    uk = Unit(a_w_in, 1024 + h * 128, ev_k, banksA)
                    if h == 0:
                        vgen[0] = v_gen(vun0)
                    while vgen[0] is not None:
                        gen_pull(vgen, 1000)
                    unit_load(uq)
                    unit_load(uk)
                    v_transposes()
                    GO = [SCR[0][:, 8:520], SCR[0][:, 1544:2056]]
                    GZ = [SCR[0][:, 520:1032], SCR[1][:, 0:512]]
                    GWf = [SCR[0][:, 1032:1544], SCR[1][:, 512:1024]]
                    gco = [["SCR0"], ["SCR0"]]
                    gcz = [["SCR0"], ["SCR1"]]
                    gcw = [["SCR0"], ["SCR1"]]

                    def gates_gen():
                        for tb in range(4):
                            for half in range(2):
                                pp = (tb * 2 + half) % 2
                                ko, kz, kw = ("GO", pp), ("GZ", pp), ("GWf", pp)
                                hg = PRT[:, 80 + 2 * h + half:80 + 2 * h + half + 1]
                                for which in range(2):
                                    gw = GW[2 * half + which]
                                    gb_ = 4 if which == 0 else 1
                                    for c in range(KC):
                                        kb.op("pe", lambda: P_.matmul(PS[gb_][:, :], lhsT=gw[:, c, :],
                                                                      rhs=XT[:, c, tb * 512:(tb + 1) * 512],
                                                                      start=(c == 0), stop=(c == KC - 1)),
                                              R=[f"GW{2 * half + which}", "XT"], W=[("ps", gb_)], inc=(c == KC - 1))
                                        if c % 4 == 3:
                                            yield None
                                    if which == 0:
                                        kb.op("act", lambda: A_.activation(out=GO[pp], in_=PS[4][:, :], func=AF.Exp, scale=-1.0),
                                              R=[("ps", 4)] + gco[pp], W=[ko])
                                    else:
                                        kb.op("act", lambda: A_.activation(out=GZ[pp], in_=PS[1][:, :], func=AF.Copy, scale=hg),
                                              R=[("ps", 1), "PRT"] + gcz[pp], W=[kz])
                                        kb.op("act", lambda: A_.activation(out=GWf[pp], in_=PS[1][:, :], func=AF.Exp, scale=-1.0),
                                              R=[("ps", 1)] + gcw[pp], W=[kw])
                                kb.op("act", lambda: A_.activation(out=GO[pp], in_=GO[pp], func=AF.Ln, bias=1.0, scale=1.0),
                                      R=[ko] + gco[pp], W=[ko])
                                kb.op("act", lambda: A_.activation(out=GWf[pp], in_=GWf[pp], func=AF.Ln, bias=1.0, scale=1.0),
                                      R=[kw] + gcw[pp], W=[kw])
                                kb.op("pool", lambda: G.tensor_tensor(out=GO[pp], in0=GO[pp], in1=GWf[pp], op=ALU.add),
                                      R=[ko, kw] + gco[pp] + gcw[pp], W=[ko])
                                yield None
                                kb.op("act", lambda: A_.activation(out=GO[pp], in_=GO[pp], func=AF.Exp, scale=-1.0),
                                      R=[ko] + gco[pp], W=[ko])
                                kb.op("pool", lambda: G.tensor_tensor(out=GT[:, half, tb * 512:(tb + 1) * 512], in0=GZ[pp],
                                                                      in1=GO[pp], op=ALU.mult),
                                      R=[kz, ko] + gco[pp] + gcz[pp], W=[("GT", tb)])
                                yield None
                            yield ("blk", tb)

                    ggen = [gates_gen()]
                    gdone = [-1]

                    def gpull(n):
                        for _ in range(n):
                            if ggen[0] is None:
                                return
                            try:
                                tok = next(ggen[0])
                            except StopIteration:
                                ggen[0] = None
                                return
                            if tok is not None:
                                gdone[0] = tok[1]

                    def gneed(tb):
                        while gdone[0] < tb and ggen[0] is not None:
                            gpull(1)

                    gneed(0)
                    kb.op("pool", lambda: G.memset(SCR[0][:, 0:3], 0.0), W=["SCR0"])
                    kb.op("pool", lambda: G.memset(SCR[1][:, 2048:2056], 0.0), W=["SCR1"])
                    unit_compute(uq, XT, "XT")
                    if h + 1 < NH_A:
                        vnext[0] = mkv(h + 1)
                        unit_load(vnext[0][0])
                    unit_compute(uk, XT, "XT")
                    if h + 1 < NH_A:
                        unit_load(vnext[0][1])
                    kb.op("pool", lambda: G.memset(SCR[0][:, 0:3], 0.0), W=["SCR0"])
                    kb.op("pool", lambda: G.memset(SCR[1][:, 2048:2056], 0.0), W=["SCR1"])

                    if h == NH_A - 1:
                        gneed(3)

                    _chk("proj%d" % h)
                    if h == NH_A - 1:
                        load_wo(a_w_out)

                    def r_front(t):
                        p = t % 2
                        col = t * 8 + h
                        tsl = slice(t * 128, (t + 1) * 128)
                        wcol = WW[:, col:col + 1]
                        kb.op("pe", lambda: P_.matmul(PS[0][:, 0:128], lhsT=KT[:, tsl], rhs=QT[:, tsl], start=True, stop=True),
                              R=["KT", "QT"], W=[("ps", 0)])
                        kb.op("dve", lambda: V.scalar_tensor_tensor(out=STM[p][:], in0=PS[0][:, 0:128], scalar=wcol, in1=MASKC[:],
                                                                    op0=ALU.mult, op1=ALU.mult),
                              R=[("ps", 0), "WW", "MASKC"], W=[f"STM{p}"])
                        kb.op("pe", lambda: P_.transpose(out=PT[:, 0:128], in_=KT[:, tsl], identity=IDB[:]),
                              R=["KT", "IDB"], W=["PT"])
                        kb.op("act", lambda: A_.activation(out=KP[p][:], in_=PT[:, 0:128], func=AF.Copy, scale=wcol),
                              R=["PT", "WW"], W=[f"KP{p}"])

                    NBK = [5, 2, 3]

                    def r_state(t):
                        p = t % 2
                        nb_ = NBK[t % 3]
                        col = t * 8 + h
                        tsl = slice(t * 128, (t + 1) * 128)
                        kb.op("pe", lambda: P_.matmul(PS[nb_][:, 0:257], lhsT=STM[p][:], rhs=VX[:, t, 0:257], start=True,
                                                      stop=(t == 0)), R=[f"STM{p}", "VX"], W=[("ps", nb_)], inc=(t == 0))
                        if t > 0:
                            kb.op("pe", lambda: P_.matmul(PS[nb_][:, 0:257], lhsT=QT[:, tsl], rhs=CB[:, 0:257], start=False,
                                                          stop=True), R=["QT", "CB"], W=[("ps", nb_)])
                        kb.op("pe", lambda: P_.matmul(PS[6][:, 0:257], lhsT=KP[p][:], rhs=VX[:, t, 0:257], start=True, stop=True),
                              R=[f"KP{p}", "VX"], W=[("ps", 6)])
                        if t == 0:
                            kb.op("dve", lambda: V.tensor_copy(out=C32[:, 0:257], in_=PS[6][:, 0:257]), R=[("ps", 6)], W=["C32"])
                        else:
                            kb.op("dve", lambda: V.scalar_tensor_tensor(out=C32[:, 0:257], in0=C32[:, 0:257],
                                                                        scalar=DB[:, col:col + 1], in1=PS[6][:, 0:257],
                                                                        op0=ALU.mult, op1=ALU.add),
                                  R=[("ps", 6), "C32", "DB"], W=["C32"])
                        if t < NT - 1:
                            kb.op("act", lambda: A_.activation(out=CB[:, 0:257], in_=C32[:, 0:257], func=AF.Copy,
                                                               scale=DB[:, col + 8:col + 9]), R=["C32", "DB"], W=["CB"])

                    def sskeys(t):
                        q = t % 3
                        ss = lambda i: SS[:, 8 * q + i:8 * q + i + 1]
                        sk = lambda i: "SS%d_%d" % (i, q)
                        return q, ss, sk

                    def r_oa(t):
                        q, ss, sk = sskeys(t)
                        nb_ = NBK[t % 3]
                        col = t * 8 + h
                        kb.op("dve", lambda: V.tensor_scalar(out=ss(6), in0=PS[nb_][:, 256:257], scalar1=-1.0, scalar2=None,
                                                             op0=ALU.mult), R=[("ps", nb_)], W=[sk(6)])
                        kb.op("dve", lambda: V.bn_stats(out=BST[:, 6 * q:6 * q + 6], in_=PS[nb_][:, 0:256]),
                              R=[("ps", nb_)], W=[f"BST{q}"])
                        kb.op("dve", lambda: V.scalar_tensor_tensor(out=ss(0), in0=PS[nb_][:, 256:257],
                                                                    scalar=FL[:, col:col + 1], in1=ss(6),
                                                                    op0=ALU.max, op1=ALU.max),
                              R=[("ps", nb_), "FL", sk(6)], W=[sk(0)])
                        kb.op("dve", lambda: V.bn_aggr(out=BMV[:, 2 * q:2 * q + 2], in_=BST[:, 6 * q:6 * q + 6]),
                              R=[f"BST{q}"], W=[f"BMV{q}"])
                        kb.op("dve", lambda: V.reciprocal(out=ss(1), in_=ss(0)), R=[sk(0)], W=[sk(1)])
                        var = BMV[:, 2 * q + 1:2 * q + 2]
                        kb.op("dve", lambda: V.scalar_tensor_tensor(out=ss(2), in0=var, scalar=ss(1), in1=ss(1),
                                                                    op0=ALU.mult, op1=ALU.mult), R=[f"BMV{q}", sk(1)], W=[sk(2)])
                        kb.op("act", lambda: A_.activation(out=ss(3), in_=ss(2), func=AF.Ln, bias=LN_EPS, scale=1.0),
                              R=[sk(2)], W=[sk(3)])
                        kb.op("act", lambda: A_.activation(out=ss(4), in_=ss(3), func=AF.Exp, scale=-0.5), R=[sk(3)], W=[sk(4)])

                    def r_ob(t):
                        q, ss, sk = sskeys(t)
                        p = t % 2
                        nb_ = NBK[t % 3]
                        mean = BMV[:, 2 * q:2 * q + 1]
                        kb.op("dve", lambda: V.tensor_tensor(out=ss(5), in0=ss(4), in1=ss(1), op=ALU.mult),
                              R=[sk(4), sk(1)], W=[sk(5)])
                        kb.op("dve", lambda: V.tensor_scalar(out=ss(7), in0=mean, scalar1=-1.0, scalar2=ss(5),
                                                             op0=ALU.mult, op1=ALU.mult), R=[f"BMV{q}", sk(5)], W=[sk(7)])
                        kb.op("act", lambda: A_.activation(out=HN[p][:], in_=PS[nb_][:, 0:256], func=AF.Identity,
                                                           bias=ss(7), scale=ss(5)),
                              R=[("ps", nb_), sk(5), sk(7)], W=[f"HN{p}"])

                    def r_oc(t):
                        p = t % 2
                        tsl = slice(t * 128, (t + 1) * 128)
                        reg = ("ps", 0)
                        for half in range(2):
                            kb.op("pe", lambda: P_.transpose(out=PS[0][:, 128 + half * 128:256 + half * 128],
                                                             in_=HN[p][:, half * 128:(half + 1) * 128],
                                                             identity=ID32[:]), R=[f"HN{p}", "ID32"], W=[reg], inc=(half == 1))
                        for half in range(2):
                            kb.op("dve", lambda: V.tensor_tensor(out=HT[:, 2 * h + half, tsl],
                                                                 in0=PS[0][:, 128 + half * 128:256 + half * 128],
                                                                 in1=GT[:, half, tsl], op=ALU.mult),
                                  R=[reg, ("GT", t // 4)], W=["MID"])

                    def start_next():
                        if gwpre[0]:
                            return
                        gwpre[0] = True
                        gcols_n = (4096 + (h + 1) * 256, 6144 + (h + 1) * 256,
                                   4096 + (h + 1) * 256 + 128, 6144 + (h + 1) * 256 + 128)
                        for gi in range(4):
                            kb.dma("pool", f"gw{gi}", GW[gi][:], wv_a[:, :, gcols_n[gi]:gcols_n[gi] + 128],
                                   W=[f"GW{gi}"])
                        vgen[0] = v_gen(vnext[0])

                    for t in range(-1, NT + 3):
                        if 0 <= t + 1 < NT:
                            r_front(t + 1)
                        if 0 <= t < NT:
                            r_state(t)
                        if 0 <= t - 1 < NT:
                            r_oa(t - 1)
                        if 0 <= t - 2 < NT:
                            r_ob(t - 2)
                        if 0 <= t - 3 < NT:
                            gneed((t - 3) // 4)
                            r_oc(t - 3)
                        gpull(5)
                        if ggen[0] is None and h + 1 < NH_A:
                            start_next()
                            gen_pull(vgen, 3)
                    gneed(3)
                    if h + 1 < NH_A:
                        start_next()
                    gwpre[0] = False
                    _chk("head%d" % h)
                kb.barrier()
              except Stop:
                eg.close()
                stopped.append(1)
            kb.barrier()
            if stopped:
                raise Stop()
            _chk("headsA")
            ln_epilogue(x_src, HT, a_w_out, a_ln_g, a_ln_b, dst, xt_out=xt_out)

        def phase_b(x_src, dst, SRC=None, skey="XT", MIDB=None, mkey="MID", preloaded=False):
            SRC = XT if SRC is None else SRC
            MIDB = HT if MIDB is None else MIDB
            if not preloaded:
                load_transpose(x_src, SRC, skey)
            kb.barrier()
            scale = 128.0 ** -0.5
            with contextlib.ExitStack() as eb:
                sa = lambda name, shape, dt: eb.enter_context(nc.sbuf_tensor(name, shape, dt))
                QTs = [sa(f"QTb{i}", [128, S], BF16) for i in range(2)]
                KTs = [sa(f"KTb{i}", [128, S], BF16) for i in range(2)]
                VVs = [sa(f"VVb{i}", [128, NT, 128], BF16) for i in range(2)]
                SZs = [sa(f"SZb{i}", [128, S], F32) for i in range(2)]
                VT = sa("VTb", [128, S], BF16)
                NB = 4
                E = [SCR[0][:, i * 512:(i + 1) * 512] for i in range(NB)]
                X = [SCR[1][:, i * 512:(i + 1) * 512] for i in range(2)]
                SP = [sa(f"SP{i}", [128, 512], BF16) for i in range(NB)]
                AT = [sa(f"AT{i}", [128, 512], BF16) for i in range(3)]
                banksB = [4, 5, 6]
                gstep = [0]

                first_loaded = {}

                def proj_gen(h):
                    par = h % 2
                    KTp, QTp, VVp, SZp = KTs[par], QTs[par], VVs[par], SZs[par]
                    kK, kQ, kV, kZ = f"KT{par}", f"QT{par}", f"VV{par}", f"SZ{par}"

                    def ev_k(tb, bank):
                        kb.op("dve", lambda: V.tensor_copy(out=KTp[:, tb * 512:(tb + 1) * 512], in_=PS[bank][:, :]),
                              R=[("ps", bank)], W=[kK])

                    def ev_q(tb, bank):
                        kb.op("dve", lambda: V.tensor_copy(out=QTp[:, tb * 512:(tb + 1) * 512], in_=PS[bank][:, :]),
                              R=[("ps", bank)], W=[kQ])

                    def ev_z(tb, bank):
                        kb.op("dve", lambda: V.tensor_copy(out=SZp[:, tb * 512:(tb + 1) * 512], in_=PS[bank][:, :]),
                              R=[("ps", bank)], W=[kZ])

                    def ev_v(tb, bank):
                        kb.op("dve", lambda: V.tensor_copy(out=VT[:, tb * 512:(tb + 1) * 512], in_=PS[bank][:, :]),
                              R=[("ps", bank)], W=["VT"])
                        if tb == 3:
                            for g in range(2):
                                for j in range(8):
                                    t = g * 8 + j
                                    kb.op("pe", lambda: P_.transpose(out=PT[:, j * 128:(j + 1) * 128],
                                                                     in_=VT[:, t * 128:(t + 1) * 128], identity=IDB[:]),
                                          R=["VT", "IDB"], W=["PT"], inc=(j == 7))
                                kb.op("dve", lambda: V.tensor_copy(out=VVp[:, g * 8:(g + 1) * 8, :],
                                                                   in_=PT[:, :].rearrange("p (a b) -> p a b", a=8)),
                                      R=["PT"], W=[kV])

                    units = [Unit(kv_w, h * 128, ev_k, banksB),
                             Unit(b_w_in, h * 128, ev_q, banksB),
                             Unit(kv_w, 2048 + h * 128, ev_v, banksB),
                             Unit(b_w_in, 2048 + h * 128, ev_z, banksB)]
                    unit_load(units[0])
                    nbk = 0
                    for i, u in enumerate(units):
                        if i + 1 < len(units):
                            unit_load(units[i + 1])
                        for tb in range(4):
                            bank = banksB[nbk % 3]
                            nbk += 1
                            for c in range(KC):
                                kb.op("pe", lambda: P_.matmul(PS[bank][:, :], lhsT=WB[u.slot][:, c, :],
                                                              rhs=SRC[:, c, tb * 512:(tb + 1) * 512],
                                                              start=(c == 0), stop=(c == KC - 1)),
                                      R=[f"WB{u.slot}", skey], W=[("ps", bank)], inc=(c == KC - 1))
                                if c % 4 == 3 and c != KC - 1:
                                    yield
                            u.evac(tb, bank)
                            yield

                def pull(gen, n):
                    if gen is None:
                        return None
                    for _ in range(n):
                        try:
                            next(gen)
                        except StopIteration:
                            return None
                    return gen

                g0 = proj_gen(0)
                while pull(g0, 1000) is not None:
                    pass
                for h in range(NH_B):
                    par = h % 2
                    KT, QT, VV, SZ = KTs[par], QTs[par], VVs[par], SZs[par]
                    kK, kQ, kV, kZ = f"KT{par}", f"QT{par}", f"VV{par}", f"SZ{par}"
                    gen = proj_gen(h + 1) if h + 1 < NH_B else None
                    if h == NH_B - 1:
                        load_wo(b_w_out, SRC, skey)
                    kb.op("act", lambda: A_.activation(out=SZ[:, :], in_=SZ[:, :], func=AF.Silu), R=[kZ], W=[kZ])
                    steps = [(gq, sg) for gq in range(4) for sg in range(4 * gq + 3, -1, -1)]
                    nst = len(steps)

                    def geo(k):
                        gq, sg = steps[k]
                        f0 = max(0, sg - 4 * gq) * 128
                        return gq, sg, f0, slice(f0, 512), (sg == 4 * gq + 3)

                    def s_front(k):
                        gq, sg, f0, fs, first = geo(k)
                        g = gstep[0] + k
                        i, zb = g % NB, g % 2
                        qs = slice(gq * 512 + f0, (gq + 1) * 512)
                        kb.op("pe", lambda: P_.matmul(PS[zb][:, fs], lhsT=KT[:, sg * 128:(sg + 1) * 128], rhs=QT[:, qs],
                                                      start=True, stop=True), R=[kK, kQ], W=[("ps", zb)])
                        kb.op("act", lambda: A_.activation(out=E[i][:, fs], in_=PS[zb][:, fs], func=AF.Exp, scale=scale),
                              R=[("ps", zb)], W=[f"E{i}"])
                        kb.op("act", lambda: A_.activation(out=SP[i][:, fs], in_=E[i][:, fs], func=AF.Ln, bias=1.0, scale=1.0),
                              R=[f"E{i}"], W=[f"SP{i}"])
                        if sg >= 4 * gq:
                            ds = slice(f0, f0 + 128)
                            kb.op("pool", lambda: G.tensor_tensor(out=SP[i][:, ds], in0=SP[i][:, ds], in1=MASKSB[:], op=ALU.mult),
                                  R=[f"SP{i}", "MASKSB"], W=[f"SP{i}"])
                            kb.op("pool", lambda: G.tensor_tensor(out=E[i][:, ds], in0=E[i][:, ds], in1=MASKS[:], op=ALU.mult),
                                  R=[f"E{i}", "MASKS"], W=[f"E{i}"])

                    def s_utri(k):
                        gq, sg, f0, fs, first = geo(k)
                        i = (gstep[0] + k) % NB
                        if sg > 0:
                            kb.op("pe", lambda: P_.matmul(PS[2][:, fs], lhsT=NUTRI[:], rhs=SP[i][:, fs], start=False,
                                                          stop=False, skip_group_check=True),
                                  R=["NUTRI", f"SP{i}"], W=[("ps", 2)])

                    def s_tri(k):
                        gq, sg, f0, fs, first = geo(k)
                        g = gstep[0] + k
                        i, xi, ai = g % NB, g % 2, g % 3
                        kb.op("pe", lambda: P_.matmul(PS[2][:, fs], lhsT=NTRI[:], rhs=SP[i][:, fs], start=first, stop=False,
                                                      skip_group_check=True), R=["NTRI", f"SP{i}"], W=[("ps", 2)])
                        kb.op("act", lambda: A_.activation(out=X[xi][:, fs], in_=PS[2][:, fs], func=AF.Exp),
                              R=[("ps", 2)], W=[f"X{xi}"])
                        kb.op("dve", lambda: V.tensor_tensor(out=AT[ai][:, fs], in0=E[i][:, fs], in1=X[xi][:, fs], op=ALU.mult),
                              R=[f"E{i}", f"X{xi}"], W=[f"AT{ai}"])

                    def s_av(k):
                        gq, sg, f0, fs, first = geo(k)
                        ai = (gstep[0] + k) % 3
                        kb.op("pe", lambda: P_.matmul(PS[3][:, fs], lhsT=VV[:, sg, :], rhs=AT[ai][:, fs], start=first,
                                                      stop=(sg == 0), skip_group_check=True),
                              R=[kV, f"AT{ai}"], W=[("ps", 3)])
                        if sg == 0:
                            kb.op("dve", lambda: V.tensor_tensor(out=MIDB[:, h, gq * 512:(gq + 1) * 512], in0=PS[3][:, :],
                                                                 in1=SZ[:, gq * 512:(gq + 1) * 512], op=ALU.mult),
                                  R=[("ps", 3), kZ], W=[mkey])

                    for k in range(-2, nst + 1):
                        if 0 <= k - 1 < nst:
                            s_utri(k - 1)
                        if 0 <= k < nst:
                            s_tri(k)
                        if 0 <= k - 1 < nst:
                            s_av(k - 1)
                        if 0 <= k + 2 < nst:
                            s_front(k + 2)
                        gen = pull(gen, 2)
                    while gen is not None:
                        gen = pull(gen, 1000)
                    gstep[0] += nst
                kb.barrier()
            kb.barrier()
            ln_epilogue(x_src, MIDB, b_w_out, b_ln_g, b_ln_b, dst, mkey=mkey, wo_buf=SRC, wo_key=skey)

        try:
            if phase == "A":
                phase_a(x_in, y_out)
            elif phase == "B":
                phase_b(x_in, y_out)
            else:
                phase_a(x_in, y_out, xt_out=True)
                phase_b(y_out, y_out, SRC=HT, skey="HTsrc", MIDB=XT, mkey="XTmid", preloaded=True)
        except Stop:
            pass
        kb.barrier()
    return nc


_NC_CACHE = {}


def _get_nc(phase):
    if phase not in _NC_CACHE:
        _NC_CACHE[phase] = build(phase)
    return _NC_CACHE[phase]


A_KEYS = ["a_w_in", "a_gate_b", "a_conv_w", "a_conv_b", "a_head_g", "a_w_out", "a_ln_g", "a_ln_b"]
B_KEYS = ["kv_w", "b_w_in", "b_w_out", "b_ln_g", "b_ln_b"]


def _prep(inputs, keys):
    out = {}
    for k in keys:
        a = np.ascontiguousarray(np.asarray(inputs[k], dtype=np.float32))
        if k in ("a_w_in", "a_w_out", "b_w_in", "b_w_out", "a_conv_w"):
            a = a[0]
            if k == "a_w_in":
                a = np.concatenate([a, np.zeros((a.shape[0], A_COLS_PAD - a.shape[1]), np.float32)], axis=1)
        elif k == "kv_w":
            pass
        else:
            a = a.reshape(1, -1)
        out[k] = np.ascontiguousarray(a)
    return out


def run_phase(phase, xs, inputs):
    keys = (A_KEYS if "A" in phase else []) + (B_KEYS if "B" in phase else [])
    w = _prep(inputs, keys)
    nc = _get_nc(phase)
    in_maps = [dict(w, x=np.ascontiguousarray(xs[i])) for i in range(8)]
    res = run_bass_kernel_spmd(nc, in_maps, core_ids=list(range(8)))
    return np.stack([r["y"] for r in res.results], axis=0)


FUSED = True


def kernel(**inputs):
    x = np.asarray(inputs["x"], dtype=np.float32)
    if FUSED:
        return run_phase("AB", x, inputs)
    x1 = run_phase("A", x, inputs)
    return run_phase("B", x1, inputs)
```

```python
import contextlib
import math
import numpy as np
import concourse.bass as bass
import concourse.mybir as mybir
from concourse.bass_utils import run_bass_kernel_spmd

F32 = mybir.dt.float32
BF16 = mybir.dt.bfloat16
AF = mybir.ActivationFunctionType
ALU = mybir.AluOpType
AX = mybir.AxisListType

S = 2048
D = 2048
KC = 16
NT = 16
NH_A = 8
NH_B = 16
A_COLS = 8208
A_COLS_PAD = 8320
ALPHA = (2.0 * 2) ** 0.25
LN_EPS = 1e-5
VW = 258


class KB:
    def __init__(self, nc, es):
        self.nc = nc
        self.es = es
        self.eng = {"pe": nc.tensor, "act": nc.scalar, "dve": nc.vector, "pool": nc.gpsimd, "sp": nc.sync}
        self.sem = {e: es.enter_context(nc.semaphore("s_" + e)) for e in ("pe", "act", "dve", "pool")}
        self.cnt = {e: 0 for e in self.sem}
        self.seen = {e: {} for e in self.eng}
        self.lastw = {}
        self.reads = {}
        self.pend = {e: ([], []) for e in self.eng}
        self.dsem = {}
        self.semobj = dict(self.sem)

    def _wait(self, e, dep):
        if dep is None:
            return
        sn, v = dep
        if self.seen[e].get(sn, 0) >= v:
            return
        self.seen[e][sn] = v
        self.eng[e].wait_ge(self.semobj[sn], v)

    def _deps(self, e, R, W):
        for k in R:
            self._wait(e, self.lastw.get(k))
        for k in W:
            self._wait(e, self.lastw.get(k))
            for d in self.reads.get(k, ()):
                self._wait(e, d)

    def _record(self, tag, R, W):
        for k in R:
            lst = self.reads.setdefault(k, [])
            lst[:] = [d for d in lst if d[0] != tag[0]]
            lst.append(tag)
        for k in W:
            self.lastw[k] = tag
            self.reads[k] = []

    @staticmethod
    def _norm(R, W):
        R2, W2 = [], []
        for k in R:
            if isinstance(k, tuple) and k[0] == "ps":
                W2.append(k[:2])
            elif k == "PT":
                W2.append(k)
            else:
                R2.append(k)
        for k in W:
            W2.append(k[:2] if (isinstance(k, tuple) and k[0] == "ps") else k)
        return R2, W2

    def op(self, e, fn, R=(), W=(), inc=True):
        R, W = self._norm(R, W)
        self._deps(e, R, W)
        ins = fn()
        if inc:
            self.cnt[e] += 1
            ins.then_inc(self.sem[e], 1)
            tag = (e, self.cnt[e])
            pr, pw = self.pend[e]
            self._record(tag, list(pr) + list(R), list(pw) + list(W))
            self.pend[e] = ([], [])
        else:
            self.pend[e][0].extend(R)
            self.pend[e][1].extend(W)
        return ins

    def dma(self, q, slot, out, in_, R=(), W=(), A=(), **kw):
        for k in A:
            self._wait(q, self.lastw.get(k))
        if slot not in self.dsem:
            s = self.es.enter_context(self.nc.semaphore("d_" + slot))
            self.dsem[slot] = [s, 0]
            self.semobj["d_" + slot] = s
        self._deps(q, R, W)
        ent = self.dsem[slot]
        ent[1] += 16
        self.eng[q].dma_start(out=out, in_=in_, **kw).then_inc(ent[0], 16)
        self._record(("d_" + slot, ent[1]), R, W)

    def barrier(self):
        for e in self.eng:
            for s in self.sem:
                if self.cnt[s] > 0:
                    self._wait(e, (s, self.cnt[s]))
            for slot, (so, c) in self.dsem.items():
                if c > 0:
                    self._wait(e, ("d_" + slot, c))


class Stop(Exception):
    pass


STOP_AT = None


def _chk(name):
    if STOP_AT == name:
        raise Stop()


def build(phase):
    nc = bass.Bass("TRN2", target_bir_lowering=False)
    dt_in = lambda name, shape: nc.dram_tensor(name, shape, F32, kind="ExternalInput").ap()
    doA = "A" in phase
    doB = "B" in phase
    x_in = dt_in("x", [S, D])
    if doA:
        a_w_in = dt_in("a_w_in", [D, A_COLS_PAD])
        a_gate_b = dt_in("a_gate_b", [1, 16])
        a_conv_w = dt_in("a_conv_w", [4, 2048])
        a_conv_b = dt_in("a_conv_b", [1, 2048])
        a_head_g = dt_in("a_head_g", [1, 2048])
        a_w_out = dt_in("a_w_out", [D, D])
        a_ln_g = dt_in("a_ln_g", [1, D])
        a_ln_b = dt_in("a_ln_b", [1, D])
    if doB:
        kv_w = dt_in("kv_w", [D, 4096])
        b_w_in = dt_in("b_w_in", [D, 4096])
        b_w_out = dt_in("b_w_out", [D, D])
        b_ln_g = dt_in("b_ln_g", [1, D])
        b_ln_b = dt_in("b_ln_b", [1, D])
    y_out = nc.dram_tensor("y", [S, D], F32, kind="ExternalOutput").ap()

    with contextlib.ExitStack() as es:
        kb = KB(nc, es)
        sb = lambda name, shape, dt: es.enter_context(nc.sbuf_tensor(name, shape, dt))
        XT = sb("XT", [128, KC, S], BF16)
        HT = sb("HT", [128, KC, S], BF16)
        WB = [sb(f"WB{i}", [128, KC, 128], BF16) for i in range(2)]
        SCR = [sb(f"SCR{i}", [128, 2056], F32) for i in range(2)]
        ID32 = sb("ID32", [128, 128], F32)
        IDB = sb("IDB", [128, 128], BF16)
        ONE32 = sb("ONE32", [128, 128], F32)
        MASKC = sb("MASKC", [128, 128], F32)
        MASKS = sb("MASKS", [128, 128], F32)
        MASKSB = sb("MASKSB", [128, 128], BF16)
        TRIU32 = sb("TRIU32", [128, 128], F32)
        NTRI = sb("NTRI", [128, 128], BF16)
        NUTRI = sb("NUTRI", [128, 128], BF16)
        STATS = sb("STATS", [128, 4, 6], F32)
        MV = sb("MV", [128, 2], F32)
        SM = sb("SM", [128, 8], F32)
        PS = [es.enter_context(nc.psum_tensor(f"PS{i}", [128, 512], F32)) for i in range(7)]
        PT = es.enter_context(nc.psum_tensor("PT", [128, 1024], BF16))

        V, G, A_, P_ = nc.vector, nc.gpsimd, nc.scalar, nc.tensor

        kb.op("pool", lambda: G.memset(ONE32[:], 1.0), W=["ONE32"])
        kb.op("pool", lambda: G.memset(SCR[0][:, 0:3], 0.0), W=["SCR0"])

        def sel(out, cmp, fill_in, base, cm, step, key):
            kb.op("pool", lambda: G.affine_select(out=out[:], in_=fill_in[:], pattern=[[step, 128]], compare_op=cmp,
                                                   fill=0.0, base=base, channel_multiplier=cm),
                  R=["ONE32"], W=[key])
        sel(ID32, ALU.is_equal, ONE32, 0, -1, 1, "ID32")
        sel(MASKC, ALU.is_ge, ONE32, 0, -1, 1, "MASKC")
        sel(MASKS, ALU.is_gt, ONE32, 0, -1, 1, "MASKS")
        sel(TRIU32, ALU.is_ge, ONE32, 0, -1, 1, "TRIU32")
        kb.op("dve", lambda: V.tensor_copy(out=IDB[:], in_=ID32[:]), R=["ID32"], W=["IDB"])
        kb.op("dve", lambda: V.tensor_copy(out=MASKSB[:], in_=MASKS[:]), R=["MASKS"], W=["MASKSB"])
        kb.op("dve", lambda: V.tensor_scalar(out=NUTRI[:], in0=MASKS[:], scalar1=-1.0, scalar2=None, op0=ALU.mult),
              R=["MASKS"], W=["NUTRI"])
        kb.op("dve", lambda: V.tensor_scalar(out=NTRI[:], in0=MASKS[:], scalar1=-1.0, scalar2=1.0, op0=ALU.add,
                                             op1=ALU.mult), R=["MASKS"], W=["NTRI"])

        def load_transpose(src, dstT, dkey):
            stg = [(SCR[0][:, 8:8 + D], "SCR0"), (SCR[1][:, 8:8 + D], "SCR1")]
            if dstT is XT:
                stg.append((HT[:, 0:2, :].rearrange("p a b -> p (a b)").bitcast(F32), "HTs0"))
                stg.append((HT[:, 2:4, :].rearrange("p a b -> p (a b)").bitcast(F32), "HTs1"))
            n = len(stg)
            for t in range(NT):
                sc, sk = stg[t % n]
                kb.dma("sp", f"xin{t % n}", sc, src[t * 128:(t + 1) * 128, :], W=[sk])
                for g in range(4):
                    bank = (t * 4 + g) % 4
                    for j in range(4):
                        c = g * 4 + j
                        kb.op("pe", lambda: P_.transpose(out=PS[bank][:, j * 128:(j + 1) * 128],
                                                         in_=sc[:, c * 128:(c + 1) * 128], identity=ID32[:]),
                              R=[sk, "ID32"], W=[("ps", bank)], inc=(j == 3))
                    dst = dstT[:, g * 4:(g + 1) * 4, t * 128:(t + 1) * 128]
                    srcv = PS[bank][:, :].rearrange("p (c t) -> p c t", c=4)
                    if (t * 4 + g) % 2 == 0:
                        kb.op("dve", lambda: V.tensor_copy(out=dst, in_=srcv), R=[("ps", bank)], W=[dkey])
                    else:
                        kb.op("act", lambda: A_.copy(out=dst, in_=srcv), R=[("ps", bank)], W=[dkey])

        wslot = [0]

        class Unit:
            def __init__(self, wsrc, col0, evac, banks):
                self.wsrc, self.col0, self.evac, self.banks = wsrc, col0, evac, banks
                self.slot = None

        def unit_load(u, after=()):
            u.slot = wslot[0] % 2
            wslot[0] += 1
            wv = u.wsrc.rearrange("(c p) n -> p c n", p=128)
            kb.dma("pool", f"wb{u.slot}", WB[u.slot][:], wv[:, :, u.col0:u.col0 + 128], A=list(after),
                   W=[f"WB{u.slot}"])

        def unit_compute(u, src, skey):
            for tb in range(4):
                bank = u.banks[tb % len(u.banks)]
                for c in range(KC):
                    kb.op("pe", lambda: P_.matmul(PS[bank][:, :], lhsT=WB[u.slot][:, c, :],
                                                  rhs=src[:, c, tb * 512:(tb + 1) * 512],
                                                  start=(c == 0), stop=(c == KC - 1)),
                          R=[f"WB{u.slot}", skey], W=[("ps", bank)], inc=(c == KC - 1))
                u.evac(tb, bank)

        def run_units(units, src, skey):
            unit_load(units[0])
            for i, u in enumerate(units):
                if i + 1 < len(units):
                    unit_load(units[i + 1])
                unit_compute(u, src, skey)

        lnid = [0]

        def load_wo(wout, buf=None, key="XT"):
            buf = XT if buf is None else buf
            wv = wout.rearrange("(c p) n -> p c n", p=128)
            for nb in range(4):
                kb.dma("pool", f"wo{nb}", buf[:, :, nb * 512:(nb + 1) * 512], wv[:, :, nb * 512:(nb + 1) * 512],
                       W=[key])

        def ln_epilogue(src_x, mid, wout, lng, lnb, dst, mkey="MID", wo_buf=None, wo_key="XT", xt_out=False):
            lnid[0] += 1
            wo_buf = XT if wo_buf is None else wo_buf
            with nc.sbuf_tensor("LNG%d" % lnid[0], [128, D], F32) as LNG, \
                    nc.sbuf_tensor("LNB%d" % lnid[0], [128, D], F32) as LNB, \
                    nc.sbuf_tensor("XB%da" % lnid[0], [128, D], BF16) as XB0, \
                    nc.sbuf_tensor("XB%db" % lnid[0], [128, D], BF16) as XB1:
                XB = [XB0, XB1]
                kb.dma("sp", "lng", LNG[:], lng.partition_broadcast(128), W=["LNG"])
                kb.dma("sp", "lnb", LNB[:], lnb.partition_broadcast(128), W=["LNB"])
                def ldx(t):
                    kb.dma("sp", f"xin{t % 2}", SCR[t % 2][:, 8:8 + D], src_x[t * 128:(t + 1) * 128, :],
                           W=[f"SCR{t % 2}"])
                pend_xt = []

                def xt_stage(tt):
                    xb, xk = XB[tt % 2], f"XB{tt % 2}"
                    for g in range(2):
                        for j in range(8):
                            c = g * 8 + j
                            kb.op("pe", lambda: P_.transpose(out=PT[:, j * 128:(j + 1) * 128],
                                                             in_=xb[:, c * 128:(c + 1) * 128], identity=IDB[:]),
                                  R=[xk, "IDB"], W=["PT"], inc=(j == 7))
                        kb.op("act", lambda: A_.copy(out=mid[:, g * 8:(g + 1) * 8, tt * 128:(tt + 1) * 128],
                                                     in_=PT[:, :].rearrange("p (a b) -> p a b", a=8)),
                              R=["PT"], W=[("X1T", tt)])

                ldx(0)
                for t in range(NT):
                    sc = SCR[t % 2]
                    sk = f"SCR{t % 2}"
                    if t + 1 < NT:
                        ldx(t + 1)
                    for nb in range(4):
                        bank = nb
                        for c in range(KC):
                            kb.op("pe", lambda: P_.matmul(PS[bank][:, :], lhsT=mid[:, c, t * 128:(t + 1) * 128],
                                                          rhs=wo_buf[:, c, nb * 512:(nb + 1) * 512],
                                                          start=(c == 0), stop=(c == KC - 1)),
                                  R=[wo_key, mkey, ("X1T", t)], W=[("ps", bank)], inc=(c == KC - 1))
                        seg = sc[:, 8 + nb * 512:8 + (nb + 1) * 512]
                        kb.op("dve", lambda: V.scalar_tensor_tensor(out=seg, in0=seg, scalar=ALPHA, in1=PS[bank][:, :],
                                                                    op0=ALU.mult, op1=ALU.add),
                              R=[sk, ("ps", bank)], W=[sk])
                        kb.op("dve", lambda: V.bn_stats(out=STATS[:, nb, :], in_=seg), R=[sk], W=["STATS"])
                    if xt_out and pend_xt:
                        xt_stage(pend_xt.pop(0))
                    kb.op("dve", lambda: V.bn_aggr(out=MV[:], in_=STATS[:].rearrange("p a b -> p (a b)")),
                          R=["STATS"], W=["MV"])
                    kb.op("act", lambda: A_.activation(out=SM[:, 0:1], in_=MV[:, 1:2], func=AF.Ln, bias=LN_EPS, scale=1.0),
                          R=["MV"], W=["SM0"])
                    kb.op("act", lambda: A_.activation(out=SM[:, 1:2], in_=SM[:, 0:1], func=AF.Exp, scale=-0.5),
                          R=["SM0"], W=["SM1"])
                    full = sc[:, 8:8 + D]
                    kb.op("dve", lambda: V.tensor_scalar(out=full, in0=full, scalar1=MV[:, 0:1], scalar2=SM[:, 1:2],
                                                         op0=ALU.subtract, op1=ALU.mult),
                          R=[sk, "MV", "SM1"], W=[sk])
                    kb.op("pool", lambda: G.tensor_tensor(out=full, in0=full, in1=LNG[:], op=ALU.mult),
                          R=[sk, "LNG"], W=[sk])
                    kb.op("dve", lambda: V.tensor_tensor(out=full, in0=full, in1=LNB[:], op=ALU.add),
                          R=[sk, "LNB"], W=[sk])
                    kb.dma("sp", f"yout{t % 2}", dst[t * 128:(t + 1) * 128, :], full, R=[sk], W=["YOUT"])
                    if xt_out:
                        xb, xk = XB[t % 2], f"XB{t % 2}"
                        kb.op("act", lambda: A_.copy(out=xb[:, :], in_=full), R=[sk], W=[xk])
                        pend_xt.append(t)
                if xt_out:
                    xt_stage(pend_xt.pop(0))
            kb.barrier()

        def phase_a(x_src, dst, xt_out=False):
            stopped = []
            with contextlib.ExitStack() as ea:
              try:
                sa = lambda name, shape, dt: ea.enter_context(nc.sbuf_tensor(name, shape, dt))
                QT = sa("QT", [128, S], BF16)
                KT = sa("KT", [128, S], BF16)
                VT = sa("VT", [128, S], BF16)
                VX = sa("VX", [128, NT, VW], BF16)
                GT = sa("GT", [128, 2, S], BF16)
                PRT = sa("PRT", [128, 96], F32)
                WW = sa("WW", [128, 128], F32)
                FL = sa("FL", [128, 128], F32)
                DB = sa("DB", [128, 128], F32)
                GW = [sa(f"GW{i}", [128, KC, 128], BF16) for i in range(4)]
                eg = contextlib.ExitStack()
                sg = lambda name, shape, dt: eg.enter_context(nc.sbuf_tensor(name, shape, dt))
                WG = sg("WG", [128, KC, 16], BF16)
                PR = sg("PR", [96, 128], F32)
                GB = sg("GB", [128, 16], F32)
                GI = sg("GI", [128, 128], F32)
                SPF = sg("SPF", [128, 128], F32)
                BP = sg("BP", [128, 128], F32)
                AA = sg("AA", [128, 128], F32)
                TMP = sg("TMP", [128, 128], F32)
                ROW = SCR[1][0:1, 0:512].rearrange("p (a b) -> p a b", a=4)
                MROW = SCR[1][0:1, 512:648]

                kb.dma("sp", "pr0", PR[0:64, :], a_conv_w.rearrange("k (c p) -> (k c) p", p=128), W=["PR"])
                kb.dma("sp", "pr1", PR[64:80, :], a_conv_b.rearrange("o (c p) -> (o c) p", p=128), W=["PR"])
                kb.dma("sp", "pr2", PR[80:96, :], a_head_g.rearrange("o (c p) -> (o c) p", p=128), W=["PR"])
                kb.dma("sp", "gb", GB[:], a_gate_b.partition_broadcast(128), W=["GB"])
                wv = a_w_in.rearrange("(c p) n -> p c n", p=128)
                kb.dma("pool", "wg", WG[:], wv[:, :, 8192:8208], W=["WG"])
                vun0 = [Unit(a_w_in, 2048, None, None), Unit(a_w_in, 2048 + 128, None, None)]
                unit_load(vun0[0], after=["WG"])
                unit_load(vun0[1], after=["WG"])
                load_transpose(x_src, XT, "XT")
                kb.barrier()
                _chk("prologue")
                for gi, c0 in enumerate((4096, 6144, 4096 + 128, 6144 + 128)):
                    kb.dma("pool", f"gw{gi}", GW[gi][:], wv[:, :, c0:c0 + 128], W=[f"GW{gi}"])
                kb.op("pe", lambda: P_.transpose(out=PS[0][:, 0:96], in_=PR[:, :], identity=ID32[0:96, 0:96]),
                      R=["PR", "ID32"], W=[("ps", 0)])
                kb.op("dve", lambda: V.tensor_copy(out=PRT[:], in_=PS[0][:, 0:96]), R=[("ps", 0)], W=["PRT"])
                kb.op("pool", lambda: G.memset(VX[:, :, 256:258], 1.0), W=["VX"])
                _chk("params")

                for t in range(NT):
                    for c in range(KC):
                        kb.op("pe", lambda: P_.matmul(PS[1][:, t * 16:(t + 1) * 16], lhsT=XT[:, c, t * 128:(t + 1) * 128],
                                                      rhs=WG[:, c, :], start=(c == 0), stop=(c == KC - 1)),
                              R=["XT", "WG"], W=[("ps", 1)], inc=(c == KC - 1 and t == NT - 1))
                g3 = PS[1][:, 0:256].rearrange("p (t g) -> p t g", g=16)
                gi3 = GI[:].rearrange("p (t h) -> p t h", h=8)
                sp3 = SPF[:].rearrange("p (t h) -> p t h", h=8)
                kb.op("dve", lambda: V.tensor_tensor(out=gi3, in0=g3[:, :, 0:8],
                                                     in1=GB[:, 0:8].unsqueeze(1).broadcast_to([128, NT, 8]), op=ALU.add),
                      R=[("ps", 1), "GB"], W=["GI"])
                kb.op("dve", lambda: V.tensor_tensor(out=sp3, in0=g3[:, :, 8:16],
                                                     in1=GB[:, 8:16].unsqueeze(1).broadcast_to([128, NT, 8]), op=ALU.add),
                      R=[("ps", 1), "GB"], W=["SPF"])
                kb.op("act", lambda: A_.activation(out=TMP[:], in_=SPF[:], func=AF.Exp, scale=-1.0), R=["SPF"], W=["TMP"])
                kb.op("act", lambda: A_.activation(out=SPF[:], in_=TMP[:], func=AF.Ln, bias=1.0, scale=1.0),
                      R=["TMP"], W=["SPF"])
                kb.op("pe", lambda: P_.matmul(PS[2][:, 0:128], lhsT=TRIU32[:], rhs=SPF[:], start=True, stop=True),
                      R=["TRIU32", "SPF"], W=[("ps", 2)])
                kb.op("dve", lambda: V.tensor_copy(out=BP[:], in_=PS[2][:, 0:128]), R=[("ps", 2)], W=["BP"])
                kb.op("dve", lambda: V.tensor_tensor(out=AA[:], in0=GI[:], in1=BP[:], op=ALU.add), R=["GI", "BP"], W=["AA"])
                kb.op("pe", lambda: P_.transpose(out=PS[3][:, 0:128], in_=AA[:], identity=ID32[:]),
                      R=["AA", "ID32"], W=[("ps", 3)])
                kb.op("dve", lambda: V.tensor_reduce(out=SM[:, 2:3], in_=PS[3][:, 0:128], axis=AX.X, op=ALU.max),
                      R=[("ps", 3)], W=["SM2"])
                kb.op("pe", lambda: P_.matmul(PS[0][0:1, 0:128], lhsT=SM[:, 2:3], rhs=ID32[:], start=True, stop=True),
                      R=["SM2", "ID32"], W=[("ps", 0)])
                kb.op("pe", lambda: P_.matmul(PS[0][0:1, 128:256], lhsT=ONE32[:, 0:1], rhs=SPF[:], start=True, stop=True),
                      R=["ONE32", "SPF"], W=[("ps", 0)])
                kb.op("dve", lambda: V.tensor_copy(out=ROW[:, 0:2, :], in_=PS[0][0:1, 0:256].rearrange("p (a b) -> p a b", a=2)),
                      R=[("ps", 0)], W=["ROW"])
                kb.op("dve", lambda: V.memset(MROW[:], 0.0), W=["MROW"])
                for t in range(NT):
                    sl = slice(t * 8, (t + 1) * 8)
                    sl1 = slice((t + 1) * 8, (t + 2) * 8)
                    kb.op("dve", lambda: V.tensor_tensor(out=ROW[:, 2, sl], in0=MROW[:, sl], in1=ROW[:, 0, sl], op=ALU.max),
                          R=["MROW", "ROW"], W=["ROW"])
                    kb.op("dve", lambda: V.tensor_tensor(out=MROW[:, sl1], in0=ROW[:, 2, sl], in1=ROW[:, 1, sl],
                                                         op=ALU.subtract), R=["ROW"], W=["MROW"])
                kb.op("dve", lambda: V.tensor_tensor(out=ROW[:, 3, :], in0=MROW[:, 0:128], in1=ROW[:, 2, :], op=ALU.subtract),
                      R=["MROW", "ROW"], W=["ROW"])
                kb.op("pe", lambda: P_.matmul(PS[2][:, 0:128], lhsT=ONE32[0:1, :], rhs=ROW[:, 2, :], start=True, stop=True),
                      R=["ONE32", "ROW"], W=[("ps", 2)])
                kb.op("pe", lambda: P_.matmul(PS[2][:, 128:256], lhsT=ONE32[0:1, :], rhs=ROW[:, 3, :], start=True, stop=True),
                      R=["ONE32", "ROW"], W=[("ps", 2)])
                kb.op("dve", lambda: V.tensor_tensor(out=TMP[:], in0=AA[:], in1=PS[2][:, 0:128], op=ALU.subtract),
                      R=["AA", ("ps", 2)], W=["TMP"])
                kb.op("act", lambda: A_.activation(out=WW[:], in_=TMP[:], func=AF.Exp, bias=-0.5 * math.log(128.0), scale=1.0),
                      R=["TMP"], W=["WW"])
                kb.op("dve", lambda: V.tensor_tensor(out=TMP[:], in0=BP[:], in1=PS[2][:, 0:128], op=ALU.subtract),
                      R=["BP", ("ps", 2), "WW"], W=["TMP"])
                kb.op("act", lambda: A_.activation(out=FL[:], in_=TMP[:], func=AF.Exp), R=["TMP"], W=["FL"])
                kb.op("act", lambda: A_.activation(out=DB[:], in_=PS[2][:, 128:256], func=AF.Exp), R=[("ps", 2)], W=["DB"])
                kb.barrier()
                eg.close()
                C32 = sa("C32", [128, VW], F32)
                CB = sa("CB", [128, VW], BF16)
                STM = [sa(f"STM{i}", [128, 128], BF16) for i in range(2)]
                KP = [sa(f"KP{i}", [128, 128], BF16) for i in range(2)]
                HN = [sa(f"HN{i}", [128, 256], F32) for i in range(2)]
                SS = sa("SS", [128, 24], F32)
                BST = sa("BST", [128, 18], F32)
                BMV = sa("BMV", [128, 6], F32)

                _chk("gates")
                U = SCR[0]
                ACC = SCR[1]

                def conv_block(ch, tb, dstT, dkey):
                    wk = lambda k: PRT[:, k * 16 + ch:k * 16 + ch + 1]
                    cbias = PRT[:, 64 + ch:64 + ch + 1]
                    accb = ACC[:, tb * 512:(tb + 1) * 512]
                    ub = lambda k: U[:, k + tb * 512:k + (tb + 1) * 512]
                    ur = ["SCR0", "SCR1", ("U", tb), "PRT"] + ([("U", tb - 1)] if tb > 0 else [])
                    kb.op("dve", lambda: V.tensor_scalar(out=accb, in0=ub(3), scalar1=wk(3), scalar2=cbias,
                                                         op0=ALU.mult, op1=ALU.add), R=ur, W=[("ACC", tb)])
                    for k in (2, 1, 0):
                        kb.op("dve", lambda: V.scalar_tensor_tensor(out=accb, in0=ub(k), scalar=wk(k), in1=accb,
                                                                    op0=ALU.mult, op1=ALU.add),
                              R=ur + [("ACC", tb)], W=[("ACC", tb)])

                def conv_silu(tb, dstT, dkey):
                    kb.op("act", lambda: A_.activation(out=dstT[:, tb * 512:(tb + 1) * 512],
                                                       in_=ACC[:, tb * 512:(tb + 1) * 512], func=AF.Silu),
                          R=["SCR1", ("ACC", tb)], W=[dkey])

                banksA = [0, 1, 2, 3]
                vnext = [None]
                VS = [VT[:, :], SCR[0][:, 8:1032].bitcast(BF16)]
                vskey = ["VT", "SCR0"]

                def v_gen(vun):
                    nbk = 0
                    for half in range(2):
                        u = vun[half]
                        for tb in range(4):
                            bank = (4, 1)[nbk % 2]
                            nbk += 1
                            for c in range(KC):
                                kb.op("pe", lambda: P_.matmul(PS[bank][:, :], lhsT=WB[u.slot][:, c, :],
                                                              rhs=XT[:, c, tb * 512:(tb + 1) * 512],
                                                              start=(c == 0), stop=(c == KC - 1)),
                                      R=[f"WB{u.slot}", "XT"], W=[("ps", bank)], inc=(c == KC - 1))
                                if c % 4 == 3:
                                    yield None
                            kb.op("act", lambda: A_.copy(out=VS[half][:, tb * 512:(tb + 1) * 512], in_=PS[bank][:, :]),
                                  R=[("ps", bank)], W=[vskey[half]])

                def v_transposes():
                    for half in range(2):
                        for g in range(2):
                            for j in range(8):
                                t = g * 8 + j
                                kb.op("pe", lambda: P_.transpose(out=PT[:, j * 128:(j + 1) * 128],
                                                                 in_=VS[half][:, t * 128:(t + 1) * 128], identity=IDB[:]),
                                      R=[vskey[half], "IDB"], W=["PT"], inc=(j == 7))
                            kb.op("dve", lambda: V.tensor_copy(
                                out=VX[:, g * 8:(g + 1) * 8, half * 128:(half + 1) * 128],
                                in_=PT[:, :].rearrange("p (a b) -> p a b", a=8)), R=["PT"], W=["VX"])
                    kb.op("pool", lambda: G.memset(SCR[0][:, 0:3], 0.0), W=["SCR0"])

                def gen_pull(g, n):
                    for _ in range(n):
                        if g[0] is None:
                            return
                        try:
                            next(g[0])
                        except StopIteration:
                            g[0] = None
                            return

                vgen = [None]
                gwpre = [False]
                for h in range(NH_A):
                    def ev_q(tb, bank):
                        kb.op("act", lambda: A_.copy(out=U[:, 3 + tb * 512:3 + (tb + 1) * 512], in_=PS[bank][:, :]),
                              R=[("ps", bank), "SCR0"], W=[("U", tb)])
                        conv_block(h, tb, QT, "QT")
                        if tb > 0:
                            conv_silu(tb - 1, QT, "QT")
                        if tb == 3:
                            conv_silu(3, QT, "QT")

                    def ev_k(tb, bank):
                        kb.op("act", lambda: A_.copy(out=U[:, 3 + tb * 512:3 + (tb + 1) * 512], in_=PS[bank][:, :]),
                              R=[("ps", bank), "SCR0"], W=[("U", tb)])
                        conv_block(8 + h, tb, KT, "KT")
                        if tb > 0:
                            conv_silu(tb - 1, KT, "KT")
                        if tb == 3:
                            conv_silu(3, KT, "KT")

                    def mk_ev_v(half):
                        def ev(tb, bank):
                            kb.op("act", lambda: A_.copy(out=VT[:, tb * 512:(tb + 1) * 512], in_=PS[bank][:, :]),
                                  R=[("ps", bank)], W=["VT"])
                            if tb == 3:
                                for g in range(2):
                                    for j in range(8):
                                        t = g * 8 + j
                                        kb.op("pe", lambda: P_.transpose(out=PT[:, j * 128:(j + 1) * 128],
                                                                         in_=VT[:, t * 128:(t + 1) * 128], identity=IDB[:]),
                                              R=["VT", "IDB"], W=["PT"], inc=(j == 7))
                                    kb.op("dve", lambda: V.tensor_copy(
                                        out=VX[:, g * 8:(g + 1) * 8, half * 128:(half + 1) * 128],
                                        in_=PT[:, :].rearrange("p (a b) -> p a b", a=8)), R=["PT"], W=["VX"])
                        return ev

                    wv_a = a_w_in.rearrange("(c p) n -> p c n", p=128)
                    gcols = (4096 + h * 256, 6144 + h * 256, 4096 + h * 256 + 128, 6144 + h * 256 + 128)

                    def gw_load(gi, after=()):
                        kb.dma("pool", f"gw{gi}", GW[gi][:], wv_a[:, :, gcols[gi]:gcols[gi] + 128],
                               A=list(after), W=[f"GW{gi}"])

                    mkv = lambda hh: [Unit(a_w_in, 2048 + hh * 256, None, banksA),
                                      Unit(a_w_in, 2048 + hh * 256 + 128, None, banksA)]
                    uq = Unit(a_w_in, h * 128, ev_q, banksA)
                    uk = Unit(a_w_in, 1024 + h * 128, ev_k, banksA)
                    if h == 0:
                        vgen[0] = v_gen(vun0)
                    while vgen[0] is not None:
                        gen_pull(vgen, 1000)
                    unit_load(uq)
                    unit_load(uk)
                    v_transposes()
                    GO = [SCR[0][:, 8:520], SCR[0][:, 1544:2056]]
                    GZ = [SCR[0][:, 520:1032], SCR[1][:, 0:512]]
                    GWf = [SCR[0][:, 1032:1544], SCR[1][:, 512:1024]]
                    gco = [["SCR0"], ["SCR0"]]
                    gcz = [["SCR0"], ["SCR1"]]
                    gcw = [["SCR0"], ["SCR1"]]

                    def gates_gen():
                        for tb in range(4):
                            for half in range(2):
                                pp = (tb * 2 + half) % 2
                                ko, kz, kw = ("GO", pp), ("GZ", pp), ("GWf", pp)
                                hg = PRT[:, 80 + 2 * h + half:80 + 2 * h + half + 1]
                                for which in range(2):
                                    gw = GW[2 * half + which]
                                    gb_ = 4 if which == 0 else 1
                                    for c in range(KC):
                                        kb.op("pe", lambda: P_.matmul(PS[gb_][:, :], lhsT=gw[:, c, :],
                                                                      rhs=XT[:, c, tb * 512:(tb + 1) * 512],
                                                                      start=(c == 0), stop=(c == KC - 1)),
                                              R=[f"GW{2 * half + which}", "XT"], W=[("ps", gb_)], inc=(c == KC - 1))
                                        if c % 4 == 3:
                                            yield None
                                    if which == 0:
                                        kb.op("act", lambda: A_.activation(out=GO[pp], in_=PS[4][:, :], func=AF.Exp, scale=-1.0),
                                              R=[("ps", 4)] + gco[pp], W=[ko])
                                    else:
                                        kb.op("act", lambda: A_.activation(out=GZ[pp], in_=PS[1][:, :], func=AF.Copy, scale=hg),
                                              R=[("ps", 1), "PRT"] + gcz[pp], W=[kz])
                                        kb.op("act", lambda: A_.activation(out=GWf[pp], in_=PS[1][:, :], func=AF.Exp, scale=-1.0),
                                              R=[("ps", 1)] + gcw[pp], W=[kw])
                                kb.op("act", lambda: A_.activation(out=GO[pp], in_=GO[pp], func=AF.Ln, bias=1.0, scale=1.0),
                                      R=[ko] + gco[pp], W=[ko])
                                kb.op("act", lambda: A_.activation(out=GWf[pp], in_=GWf[pp], func=AF.Ln, bias=1.0, scale=1.0),
                                      R=[kw] + gcw[pp], W=[kw])
                                kb.op("pool", lambda: G.tensor_tensor(out=GO[pp], in0=GO[pp], in1=GWf[pp], op=ALU.add),
                                      R=[ko, kw] + gco[pp] + gcw[pp], W=[ko])
                                yield None
                                kb.op("act", lambda: A_.activation(out=GO[pp], in_=GO[pp], func=AF.Exp, scale=-1.0),
                                      R=[ko] + gco[pp], W=[ko])
                                kb.op("pool", lambda: G.tensor_tensor(out=GT[:, half, tb * 512:(tb + 1) * 512], in0=GZ[pp],
                                                                      in1=GO[pp], op=ALU.mult),
                                      R=[kz, ko] + gco[pp] + gcz[pp], W=[("GT", tb)])
                                yield None
                            yield ("blk", tb)

                    ggen = [gates_gen()]
                    gdone = [-1]

                    def gpull(n):
                        for _ in range(n):
                            if ggen[0] is None:
                                return
                            try:
                                tok = next(ggen[0])
                            except StopIteration:
                                ggen[0] = None
                                return
                            if tok is not None:
                                gdone[0] = tok[1]

                    def gneed(tb):
                        while gdone[0] < tb and ggen[0] is not None:
                            gpull(1)

                    gneed(0)
                    kb.op("pool", lambda: G.memset(SCR[0][:, 0:3], 0.0), W=["SCR0"])
                    kb.op("pool", lambda: G.memset(SCR[1][:, 2048:2056], 0.0), W=["SCR1"])
                    unit_compute(uq, XT, "XT")
                    if h + 1 < NH_A:
                        vnext[0] = mkv(h + 1)
                        unit_load(vnext[0][0])
                    unit_compute(uk, XT, "XT")
                    if h + 1 < NH_A:
                        unit_load(vnext[0][1])
                    kb.op("pool", lambda: G.memset(SCR[0][:, 0:3], 0.0), W=["SCR0"])
                    kb.op("pool", lambda: G.memset(SCR[1][:, 2048:2056], 0.0), W=["SCR1"])

                    if h == NH_A - 1:
                        gneed(3)

                    _chk("proj%d" % h)
                    if h == NH_A - 1:
                        load_wo(a_w_out)

                    def r_front(t):
                        p = t % 2
                        col = t * 8 + h
                        tsl = slice(t * 128, (t + 1) * 128)
                        wcol = WW[:, col:col + 1]
                        kb.op("pe", lambda: P_.matmul(PS[0][:, 0:128], lhsT=KT[:, tsl], rhs=QT[:, tsl], start=True, stop=True),
                              R=["KT", "QT"], W=[("ps", 0)])
                        kb.op("dve", lambda: V.scalar_tensor_tensor(out=STM[p][:], in0=PS[0][:, 0:128], scalar=wcol, in1=MASKC[:],
                                                                    op0=ALU.mult, op1=ALU.mult),
                              R=[("ps", 0), "WW", "MASKC"], W=[f"STM{p}"])
                        kb.op("pe", lambda: P_.transpose(out=PT[:, 0:128], in_=KT[:, tsl], identity=IDB[:]),
                              R=["KT", "IDB"], W=["PT"])
                        kb.op("act", lambda: A_.activation(out=KP[p][:], in_=PT[:, 0:128], func=AF.Copy, scale=wcol),
                              R=["PT", "WW"], W=[f"KP{p}"])

                    NBK = [5, 2, 3]

                    def r_state(t):
                        p = t % 2
                        nb_ = NBK[t % 3]
                        col = t * 8 + h
                        tsl = slice(t * 128, (t + 1) * 128)
                        kb.op("pe", lambda: P_.matmul(PS[nb_][:, 0:257], lhsT=STM[p][:], rhs=VX[:, t, 0:257], start=True,
                                                      stop=(t == 0)), R=[f"STM{p}", "VX"], W=[("ps", nb_)], inc=(t == 0))
                        if t > 0:
                            kb.op("pe", lambda: P_.matmul(PS[nb_][:, 0:257], lhsT=QT[:, tsl], rhs=CB[:, 0:257], start=False,
                                                          stop=True), R=["QT", "CB"], W=[("ps", nb_)])
                        kb.op("pe", lambda: P_.matmul(PS[6][:, 0:257], lhsT=KP[p][:], rhs=VX[:, t, 0:257], start=True, stop=True),
                              R=[f"KP{p}", "VX"], W=[("ps", 6)])
                        if t == 0:
                            kb.op("dve", lambda: V.tensor_copy(out=C32[:, 0:257], in_=PS[6][:, 0:257]), R=[("ps", 6)], W=["C32"])
                        else:
                            kb.op("dve", lambda: V.scalar_tensor_tensor(out=C32[:, 0:257], in0=C32[:, 0:257],
                                                                        scalar=DB[:, col:col + 1], in1=PS[6][:, 0:257],
                                                                        op0=ALU.mult, op1=ALU.add),
                                  R=[("ps", 6), "C32", "DB"], W=["C32"])
                        if t < NT - 1:
                            kb.op("act", lambda: A_.activation(out=CB[:, 0:257], in_=C32[:, 0:257], func=AF.Copy,
                                                               scale=DB[:, col + 8:col + 9]), R=["C32", "DB"], W=["CB"])

                    def sskeys(t):
                        q = t % 3
                        ss = lambda i: SS[:, 8 * q + i:8 * q + i + 1]
                        sk = lambda i: "SS%d_%d" % (i, q)
                        return q, ss, sk

                    def r_oa(t):
                        q, ss, sk = sskeys(t)
                        nb_ = NBK[t % 3]
                        col = t * 8 + h
                        kb.op("dve", lambda: V.tensor_scalar(out=ss(6), in0=PS[nb_][:, 256:257], scalar1=-1.0, scalar2=None,
                                                             op0=ALU.mult), R=[("ps", nb_)], W=[sk(6)])
                        kb.op("dve", lambda: V.bn_stats(out=BST[:, 6 * q:6 * q + 6], in_=PS[nb_][:, 0:256]),
                              R=[("ps", nb_)], W=[f"BST{q}"])
                        kb.op("dve", lambda: V.scalar_tensor_tensor(out=ss(0), in0=PS[nb_][:, 256:257],
                                                                    scalar=FL[:, col:col + 1], in1=ss(6),
                                                                    op0=ALU.max, op1=ALU.max),
                              R=[("ps", nb_), "FL", sk(6)], W=[sk(0)])
                        kb.op("dve", lambda: V.bn_aggr(out=BMV[:, 2 * q:2 * q + 2], in_=BST[:, 6 * q:6 * q + 6]),
                              R=[f"BST{q}"], W=[f"BMV{q}"])
                        kb.op("dve", lambda: V.reciprocal(out=ss(1), in_=ss(0)), R=[sk(0)], W=[sk(1)])
                        var = BMV[:, 2 * q + 1:2 * q + 2]
                        kb.op("dve", lambda: V.scalar_tensor_tensor(out=ss(2), in0=var, scalar=ss(1), in1=ss(1),
                                                                    op0=ALU.mult, op1=ALU.mult), R=[f"BMV{q}", sk(1)], W=[sk(2)])
                        kb.op("act", lambda: A_.activation(out=ss(3), in_=ss(2), func=AF.Ln, bias=LN_EPS, scale=1.0),
                              R=[sk(2)], W=[sk(3)])
                        kb.op("act", lambda: A_.activation(out=ss(4), in_=ss(3), func=AF.Exp, scale=-0.5), R=[sk(3)], W=[sk(4)])

                    def r_ob(t):
                        q, ss, sk = sskeys(t)
                        p = t % 2
                        nb_ = NBK[t % 3]
                        mean = BMV[:, 2 * q:2 * q + 1]
                        kb.op("dve", lambda: V.tensor_tensor(out=ss(5), in0=ss(4), in1=ss(1), op=ALU.mult),
                              R=[sk(4), sk(1)], W=[sk(5)])
                        kb.op("dve", lambda: V.tensor_scalar(out=ss(7), in0=mean, scalar1=-1.0, scalar2=ss(5),
                                                             op0=ALU.mult, op1=ALU.mult), R=[f"BMV{q}", sk(5)], W=[sk(7)])
                        kb.op("act", lambda: A_.activation(out=HN[p][:], in_=PS[nb_][:, 0:256], func=AF.Identity,
                                                           bias=ss(7), scale=ss(5)),
                              R=[("ps", nb_), sk(5), sk(7)], W=[f"HN{p}"])

                    def r_oc(t):
                        p = t % 2
                        tsl = slice(t * 128, (t + 1) * 128)
                        reg = ("ps", 0)
                        for half in range(2):
                            kb.op("pe", lambda: P_.transpose(out=PS[0][:, 128 + half * 128:256 + half * 128],
                                                             in_=HN[p][:, half * 128:(half + 1) * 128],
                                                             identity=ID32[:]), R=[f"HN{p}", "ID32"], W=[reg], inc=(half == 1))
                        for half in range(2):
                            kb.op("dve", lambda: V.tensor_tensor(out=HT[:, 2 * h + half, tsl],
                                                                 in0=PS[0][:, 128 + half * 128:256 + half * 128],
                                                                 in1=GT[:, half, tsl], op=ALU.mult),
                                  R=[reg, ("GT", t // 4)], W=["MID"])

                    def start_next():
                        if gwpre[0]:
                            return
                        gwpre[0] = True
                        gcols_n = (4096 + (h + 1) * 256, 6144 + (h + 1) * 256,
                                   4096 + (h + 1) * 256 + 128, 6144 + (h + 1) * 256 + 128)
                        for gi in range(4):
                            kb.dma("pool", f"gw{gi}", GW[gi][:], wv_a[:, :, gcols_n[gi]:gcols_n[gi] + 128],
                                   W=[f"GW{gi}"])
                        vgen[0] = v_gen(vnext[0])

                    for t in range(-1, NT + 3):
                        if 0 <= t + 1 < NT:
                            r_front(t + 1)
                        if 0 <= t < NT:
                            r_state(t)
                        if 0 <= t - 1 < NT:
                            r_oa(t - 1)
                        if 0 <= t - 2 < NT:
                            r_ob(t - 2)
                        if 0 <= t - 3 < NT:
                            gneed((t - 3) // 4)
                            r_oc(t - 3)
                        gpull(6)
                        if ggen[0] is None and h + 1 < NH_A:
                            start_next()
                            gen_pull(vgen, 3)
                    gneed(3)
                    if h + 1 < NH_A:
                        start_next()
                    gwpre[0] = False
                    _chk("head%d" % h)
                kb.barrier()
              except Stop:
                eg.close()
                stopped.append(1)
            kb.barrier()
            if stopped:
                raise Stop()
            _chk("headsA")
            ln_epilogue(x_src, HT, a_w_out, a_ln_g, a_ln_b, dst, xt_out=xt_out)

        def phase_b(x_src, dst, SRC=None, skey="XT", MIDB=None, mkey="MID", preloaded=False):
            SRC = XT if SRC is None else SRC
            MIDB = HT if MIDB is None else MIDB
            if not preloaded:
                load_transpose(x_src, SRC, skey)
            kb.barrier()
            scale = 128.0 ** -0.5
            with contextlib.ExitStack() as eb:
                sa = lambda name, shape, dt: eb.enter_context(nc.sbuf_tensor(name, shape, dt))
                QTs = [sa(f"QTb{i}", [128, S], BF16) for i in range(2)]
                KTs = [sa(f"KTb{i}", [128, S], BF16) for i in range(2)]
                VVs = [sa(f"VVb{i}", [128, NT, 128], BF16) for i in range(2)]
                SZs = [sa(f"SZb{i}", [128, S], F32) for i in range(2)]
                VT = sa("VTb", [128, S], BF16)
                NB = 4
                E = [SCR[0][:, i * 512:(i + 1) * 512] for i in range(NB)]
                X = [SCR[1][:, i * 512:(i + 1) * 512] for i in range(2)]
                SP = [sa(f"SP{i}", [128, 512], BF16) for i in range(NB)]
                AT = [sa(f"AT{i}", [128, 512], BF16) for i in range(3)]
                banksB = [4, 5, 6]
                gstep = [0]

                first_loaded = {}

                def proj_gen(h):
                    par = h % 2
                    KTp, QTp, VVp, SZp = KTs[par], QTs[par], VVs[par], SZs[par]
                    kK, kQ, kV, kZ = f"KT{par}", f"QT{par}", f"VV{par}", f"SZ{par}"

                    def ev_k(tb, bank):
                        kb.op("dve", lambda: V.tensor_copy(out=KTp[:, tb * 512:(tb + 1) * 512], in_=PS[bank][:, :]),
                              R=[("ps", bank)], W=[kK])

                    def ev_q(tb, bank):
                        kb.op("dve", lambda: V.tensor_copy(out=QTp[:, tb * 512:(tb + 1) * 512], in_=PS[bank][:, :]),
                              R=[("ps", bank)], W=[kQ])

                    def ev_z(tb, bank):
                        kb.op("dve", lambda: V.tensor_copy(out=SZp[:, tb * 512:(tb + 1) * 512], in_=PS[bank][:, :]),
                              R=[("ps", bank)], W=[kZ])

                    def ev_v(tb, bank):
                        kb.op("dve", lambda: V.tensor_copy(out=VT[:, tb * 512:(tb + 1) * 512], in_=PS[bank][:, :]),
                              R=[("ps", bank)], W=["VT"])
                        if tb == 3:
                            for g in range(2):
                                for j in range(8):
                                    t = g * 8 + j
                                    kb.op("pe", lambda: P_.transpose(out=PT[:, j * 128:(j + 1) * 128],
                                                                     in_=VT[:, t * 128:(t + 1) * 128], identity=IDB[:]),
                                          R=["VT", "IDB"], W=["PT"], inc=(j == 7))
                                kb.op("dve", lambda: V.tensor_copy(out=VVp[:, g * 8:(g + 1) * 8, :],
                                                                   in_=PT[:, :].rearrange("p (a b) -> p a b", a=8)),
                                      R=["PT"], W=[kV])

                    units = [Unit(kv_w, h * 128, ev_k, banksB),
                             Unit(b_w_in, h * 128, ev_q, banksB),
                             Unit(kv_w, 2048 + h * 128, ev_v, banksB),
                             Unit(b_w_in, 2048 + h * 128, ev_z, banksB)]
                    unit_load(units[0])
                    nbk = 0
                    for i, u in enumerate(units):
                        if i + 1 < len(units):
                            unit_load(units[i + 1])
                        for tb in range(4):
                            bank = banksB[nbk % 3]
                            nbk += 1
                            for c in range(KC):
                                kb.op("pe", lambda: P_.matmul(PS[bank][:, :], lhsT=WB[u.slot][:, c, :],
                                                              rhs=SRC[:, c, tb * 512:(tb + 1) * 512],
                                                              start=(c == 0), stop=(c == KC - 1)),
                                      R=[f"WB{u.slot}", skey], W=[("ps", bank)], inc=(c == KC - 1))
                                if c % 4 == 3 and c != KC - 1:
                                    yield
                            u.evac(tb, bank)
                            yield

                def pull(gen, n):
                    if gen is None:
                        return None
                    for _ in range(n):
                        try:
                            next(gen)
                        except StopIteration:
                            return None
                    return gen

                g0 = proj_gen(0)
                while pull(g0, 1000) is not None:
                    pass
                for h in range(NH_B):
                    par = h % 2
                    KT, QT, VV, SZ = KTs[par], QTs[par], VVs[par], SZs[par]
                    kK, kQ, kV, kZ = f"KT{par}", f"QT{par}", f"VV{par}", f"SZ{par}"
                    gen = proj_gen(h + 1) if h + 1 < NH_B else None
                    if h == NH_B - 1:
                        load_wo(b_w_out, SRC, skey)
                    kb.op("act", lambda: A_.activation(out=SZ[:, :], in_=SZ[:, :], func=AF.Silu), R=[kZ], W=[kZ])
                    steps = [(gq, sg) for gq in range(4) for sg in range(4 * gq + 3, -1, -1)]
                    nst = len(steps)

                    def geo(k):
                        gq, sg = steps[k]
                        f0 = max(0, sg - 4 * gq) * 128
                        return gq, sg, f0, slice(f0, 512), (sg == 4 * gq + 3)

                    def s_front(k):
                        gq, sg, f0, fs, first = geo(k)
                        g = gstep[0] + k
                        i, zb = g % NB, g % 2
                        qs = slice(gq * 512 + f0, (gq + 1) * 512)
                        kb.op("pe", lambda: P_.matmul(PS[zb][:, fs], lhsT=KT[:, sg * 128:(sg + 1) * 128], rhs=QT[:, qs],
                                                      start=True, stop=True), R=[kK, kQ], W=[("ps", zb)])
                        kb.op("act", lambda: A_.activation(out=E[i][:, fs], in_=PS[zb][:, fs], func=AF.Exp, scale=scale),
                              R=[("ps", zb)], W=[f"E{i}"])
                        kb.op("act", lambda: A_.activation(out=SP[i][:, fs], in_=E[i][:, fs], func=AF.Ln, bias=1.0, scale=1.0),
                              R=[f"E{i}"], W=[f"SP{i}"])
                        if sg >= 4 * gq:
                            ds = slice(f0, f0 + 128)
                            kb.op("pool", lambda: G.tensor_tensor(out=SP[i][:, ds], in0=SP[i][:, ds], in1=MASKSB[:], op=ALU.mult),
                                  R=[f"SP{i}", "MASKSB"], W=[f"SP{i}"])
                            kb.op("pool", lambda: G.tensor_tensor(out=E[i][:, ds], in0=E[i][:, ds], in1=MASKS[:], op=ALU.mult),
                                  R=[f"E{i}", "MASKS"], W=[f"E{i}"])

                    def s_utri(k):
                        gq, sg, f0, fs, first = geo(k)
                        i = (gstep[0] + k) % NB
                        if sg > 0:
                            kb.op("pe", lambda: P_.matmul(PS[2][:, fs], lhsT=NUTRI[:], rhs=SP[i][:, fs], start=False,
                                                          stop=False, skip_group_check=True),
                                  R=["NUTRI", f"SP{i}"], W=[("ps", 2)])

                    def s_tri(k):
                        gq, sg, f0, fs, first = geo(k)
                        g = gstep[0] + k
                        i, xi, ai = g % NB, g % 2, g % 3
                        kb.op("pe", lambda: P_.matmul(PS[2][:, fs], lhsT=NTRI[:], rhs=SP[i][:, fs], start=first, stop=False,
                                                      skip_group_check=True), R=["NTRI", f"SP{i}"], W=[("ps", 2)])
                        kb.op("act", lambda: A_.activation(out=X[xi][:, fs], in_=PS[2][:, fs], func=AF.Exp),
                              R=[("ps", 2)], W=[f"X{xi}"])
                        kb.op("dve", lambda: V.tensor_tensor(out=AT[ai][:, fs], in0=E[i][:, fs], in1=X[xi][:, fs], op=ALU.mult),
                              R=[f"E{i}", f"X{xi}"], W=[f"AT{ai}"])

                    def s_av(k):
                        gq, sg, f0, fs, first = geo(k)
                        ai = (gstep[0] + k) % 3
                        kb.op("pe", lambda: P_.matmul(PS[3][:, fs], lhsT=VV[:, sg, :], rhs=AT[ai][:, fs], start=first,
                                                      stop=(sg == 0), skip_group_check=True),
                              R=[kV, f"AT{ai}"], W=[("ps", 3)])
                        if sg == 0:
                            kb.op("dve", lambda: V.tensor_tensor(out=MIDB[:, h, gq * 512:(gq + 1) * 512], in0=PS[3][:, :],
                                                                 in1=SZ[:, gq * 512:(gq + 1) * 512], op=ALU.mult),
                                  R=[("ps", 3), kZ], W=[mkey])

                    for k in range(-2, nst + 1):
                        if 0 <= k - 1 < nst:
                            s_utri(k - 1)
                        if 0 <= k < nst:
                            s_tri(k)
                        if 0 <= k - 1 < nst:
                            s_av(k - 1)
                        if 0 <= k + 2 < nst:
                            s_front(k + 2)
                        gen = pull(gen, 2)
                    while gen is not None:
                        gen = pull(gen, 1000)
                    gstep[0] += nst
                kb.barrier()
            kb.barrier()
            ln_epilogue(x_src, MIDB, b_w_out, b_ln_g, b_ln_b, dst, mkey=mkey, wo_buf=SRC, wo_key=skey)

        try:
            if phase == "A":
                phase_a(x_in, y_out)
            elif phase == "B":
                phase_b(x_in, y_out)
            else:
                phase_a(x_in, y_out, xt_out=True)
                phase_b(y_out, y_out, SRC=HT, skey="HTsrc", MIDB=XT, mkey="XTmid", preloaded=True)
        except Stop:
            pass
        kb.barrier()
    return nc


_NC_CACHE = {}


def _get_nc(phase):
    if phase not in _NC_CACHE:
        _NC_CACHE[phase] = build(phase)
    return _NC_CACHE[phase]


A_KEYS = ["a_w_in", "a_gate_b", "a_conv_w", "a_conv_b", "a_head_g", "a_w_out", "a_ln_g", "a_ln_b"]
B_KEYS = ["kv_w", "b_w_in", "b_w_out", "b_ln_g", "b_ln_b"]


def _prep(inputs, keys):
    out = {}
    for k in keys:
        a = np.ascontiguousarray(np.asarray(inputs[k], dtype=np.float32))
        if k in ("a_w_in", "a_w_out", "b_w_in", "b_w_out", "a_conv_w"):
            a = a[0]
            if k == "a_w_in":
                a = np.concatenate([a, np.zeros((a.shape[0], A_COLS_PAD - a.shape[1]), np.float32)], axis=1)
        elif k == "kv_w":
            pass
        else:
            a = a.reshape(1, -1)
        out[k] = np.ascontiguousarray(a)
    return out


def run_phase(phase, xs, inputs):
    keys = (A_KEYS if "A" in phase else []) + (B_KEYS if "B" in phase else [])
    w = _prep(inputs, keys)
    nc = _get_nc(phase)
    in_maps = [dict(w, x=np.ascontiguousarray(xs[i])) for i in range(8)]
    res = run_bass_kernel_spmd(nc, in_maps, core_ids=list(range(8)))
    return np.stack([r["y"] for r in res.results], axis=0)


FUSED = True


def kernel(**inputs):
    x = np.asarray(inputs["x"], dtype=np.float32)
    if FUSED:
        return run_phase("AB", x, inputs)
    x1 = run_phase("A", x, inputs)
    return run_phase("B", x1, inputs)
```
